# Optimizing a Trainium2 kernel written in Bass

```python
import math
import jax, jax.numpy as jnp
from jax import lax
import numpy as np

D_MODEL = 1024
BATCH = 8
SEQ = 2048
DEPTH = 4

N_MIXERS = 2
N_RET_LAYERS = (DEPTH + 1) // 2
N_MLA_LAYERS = DEPTH // 2
MIX_WIDTH = D_MODEL
MEM_LEN = 256
MEM_HEADS = 4
MEM_HEAD_DIM = 64
MEM_WIDTH = MEM_HEADS * MEM_HEAD_DIM
TOK_WIDTH = MIX_WIDTH - MEM_WIDTH
RET_HEADS = 6
RET_DK = 128
RET_DV = TOK_WIDTH // RET_HEADS
RET_CHUNK = 128
RET_IN = 4 * TOK_WIDTH + MEM_WIDTH
MLA_HEADS = 6
MLA_NOPE = 64
MLA_ROPE = 32
MLA_DV = TOK_WIDTH // MLA_HEADS
MLA_Q_RANK = 256
MLA_KV_RANK = 128
MLA_IN = MLA_Q_RANK + MLA_KV_RANK + MLA_ROPE + MEM_WIDTH
MLA_SCALE = (MLA_NOPE + MLA_ROPE) ** -0.5
Q_BLOCK = 128
D_FF = 2816
FFN_RES_WEIGHT = 0.5
ROPE_THETA = 10000.0
NORM_EPS = 1e-6

kernel_name = "hybrid_retention_mla_memory_macaron_encoder"


def rms_norm(x, g):
    xf = x.astype(jnp.float32)
    y = xf * lax.rsqrt(jnp.mean(xf * xf, axis=-1, keepdims=True) + NORM_EPS)
    return (y * g.astype(jnp.float32)).astype(x.dtype)


def head_layer_norm(y, g):
    mu = jnp.mean(y, axis=-1, keepdims=True)
    var = jnp.mean(jnp.square(y - mu), axis=-1, keepdims=True)
    return (y - mu) * lax.rsqrt(var + NORM_EPS) * g.astype(jnp.float32)


def rope_tables(positions, dim):
    inv = 1.0 / (ROPE_THETA ** (jnp.arange(0, dim, 2, dtype=jnp.float32) / dim))
    ang = positions.astype(jnp.float32)[..., None] * inv
    return jnp.cos(ang), jnp.sin(ang)


def apply_rope(x, cos, sin):
    half = x.shape[-1] // 2
    x1, x2 = x[..., :half], x[..., half:]
    c = cos[:, :, None, :].astype(x.dtype)
    s = sin[:, :, None, :].astype(x.dtype)
    return jnp.concatenate([x1 * c - x2 * s, x1 * s + x2 * c], axis=-1)


def swiglu(h, w_gu, w_down):
    g, u = jnp.split(h @ w_gu, 2, axis=-1)
    return (jax.nn.silu(g) * u) @ w_down


def retention_chunkwise(q, k, v, log_gamma, strict):
    B, S, H, dk = q.shape
    dv = v.shape[-1]
    n = S // RET_CHUNK
    C = RET_CHUNK
    q = q.astype(jnp.float32).reshape(B, n, C, H, dk)
    k = k.astype(jnp.float32).reshape(B, n, C, H, dk)
    v = v.astype(jnp.float32).reshape(B, n, C, H, dv)
    lg = log_gamma.astype(jnp.float32)
    idx = jnp.arange(C, dtype=jnp.float32)
    diff = idx[:, None] - idx[None, :]
    mask = diff > 0 if strict else diff >= 0
    decay = jnp.where(mask[None], jnp.exp(jnp.where(mask, diff, 0.0)[None] * lg[:, None, None]), 0.0)
    scores = jnp.einsum('bnihd,bnjhd->bnhij', q, k) * decay[None, None]
    intra = jnp.einsum('bnhij,bnjhv->bnihv', scores, v)
    k_w = k * jnp.exp((C - 1.0 - idx)[:, None] * lg[None, :])[None, None, :, :, None]
    chunk_kv = jnp.einsum('bnjhd,bnjhv->bnhdv', k_w, v)
    g_chunk = jnp.exp(C * lg)[None, :, None, None]

    def step(state, kv_n):
        return state * g_chunk + kv_n, state

    init = jnp.zeros((B, H, dk, dv), jnp.float32)
    _, states = lax.scan(step, init, jnp.moveaxis(chunk_kv, 1, 0))
    states = jnp.moveaxis(states, 0, 1)
    q_w = q * jnp.exp((idx + 1.0)[:, None] * lg[None, :])[None, None, :, :, None]
    cross = jnp.einsum('bnihd,bnhdv->bnihv', q_w, states)
    return (intra + cross).reshape(B, S, H, dv)


def retention_mixer(h, w_in, log_decay, head_norm, cos, sin):
    B, S, _ = h.shape
    T = TOK_WIDTH
    q, k, v, g, qm = jnp.split(h @ w_in, [T, 2 * T, 3 * T, 4 * T], axis=-1)
    q = apply_rope(q.reshape(B, S, RET_HEADS, RET_DK), cos, sin)
    k = apply_rope(k.reshape(B, S, RET_HEADS, RET_DK), cos, sin) * (RET_DK ** -0.5)
    v = v.reshape(B, S, RET_HEADS, RET_DV)
    y_fwd = retention_chunkwise(q, k, v, log_decay[0], strict=False)
    y_bwd = retention_chunkwise(q[:, ::-1], k[:, ::-1], v[:, ::-1], log_decay[1], strict=True)[:, ::-1]
    y = head_layer_norm(y_fwd + y_bwd, head_norm).reshape(B, S, T).astype(h.dtype)
    return jax.nn.silu(g) * y, qm


def dense_attention_blocked(q, k, v, scale):
    B, S, H, dq = q.shape
    dv = v.shape[-1]
    n = S // Q_BLOCK
    qb = q.reshape(B, n, Q_BLOCK, H, dq).transpose(1, 0, 2, 3, 4)

    def one_block(q_blk):
        s = jnp.einsum('bqhd,bkhd->bhqk', q_blk, k).astype(jnp.float32) * scale
        p = jax.nn.softmax(s, axis=-1).astype(v.dtype)
        return jnp.einsum('bhqk,bkhv->bqhv', p, v)

    o = lax.map(one_block, qb)
    return o.transpose(1, 0, 2, 3, 4).reshape(B, S, H * dv)


def mla_mixer(h, w_in, q_norm, kv_norm, w_uq, w_ukv, cos, sin):
    B, S, _ = h.shape
    c_q, c_kv, k_r, qm = jnp.split(
        h @ w_in, [MLA_Q_RANK, MLA_Q_RANK + MLA_KV_RANK, MLA_Q_RANK + MLA_KV_RANK + MLA_ROPE], axis=-1)
    c_q = rms_norm(c_q, q_norm)
    c_kv = rms_norm(c_kv, kv_norm)
    q = (c_q @ w_uq).reshape(B, S, MLA_HEADS, MLA_NOPE + MLA_ROPE)
    q = jnp.concatenate([q[..., :MLA_NOPE], apply_rope(q[..., MLA_NOPE:], cos, sin)], axis=-1)
    kv = (c_kv @ w_ukv).reshape(B, S, MLA_HEADS, MLA_NOPE + MLA_DV)
    k_nope, v = kv[..., :MLA_NOPE], kv[..., MLA_NOPE:]
    k_rope = apply_rope(k_r.reshape(B, S, 1, MLA_ROPE), cos, sin)
    k = jnp.concatenate([k_nope, jnp.broadcast_to(k_rope, (B, S, MLA_HEADS, MLA_ROPE))], axis=-1)
    return dense_attention_blocked(q, k, v, MLA_SCALE), qm


def memory_attention(qm, mem_n, w_kv):
    B, S, _ = qm.shape
    L = mem_n.shape[1]
    q = qm.reshape(B, S, MEM_HEADS, MEM_HEAD_DIM)
    mk, mv = jnp.split(mem_n @ w_kv, 2, axis=-1)
    mk = mk.reshape(B, L, MEM_HEADS, MEM_HEAD_DIM)
    mv = mv.reshape(B, L, MEM_HEADS, MEM_HEAD_DIM)
    s = jnp.einsum('bshd,bmhd->bhsm', q, mk).astype(jnp.float32) * (MEM_HEAD_DIM ** -0.5)
    p = jax.nn.softmax(s, axis=-1).astype(mv.dtype)
    return jnp.einsum('bhsm,bmhd->bshd', p, mv).reshape(B, S, MEM_WIDTH)


def setup_inputs(seed: int = 0) -> dict:
    key = jax.random.key(seed)
    ks = jax.random.split(key, 17)
    f32 = jnp.float32

    def w(k, shape, fan_in):
        return jax.random.normal(k, shape, f32) * (fan_in ** -0.5)

    def gain(k, shape):
        return 1.0 + 0.02 * jax.random.normal(k, shape, f32)

    x = jax.random.normal(ks[0], (BATCH, SEQ, D_MODEL), f32)
    mem = jax.random.normal(ks[1], (BATCH, MEM_LEN, D_MODEL), f32)
    offsets = jax.random.randint(ks[2], (BATCH, 1), 0, 1024, dtype=jnp.int32)
    positions = (offsets + jnp.arange(SEQ, dtype=jnp.int32)[None, :]).astype(jnp.int32)
    base_decay = jnp.log(1.0 - 2.0 ** (-5.0 - jnp.arange(RET_HEADS, dtype=f32)))
    ret_log_decay = base_decay[None, None, :] * (1.0 + 0.05 * jax.random.normal(ks[10], (N_RET_LAYERS, 2, RET_HEADS), f32))
    return {
        "x": x,
        "mem": mem,
        "positions": positions,
        "norm_gains": gain(ks[3], (DEPTH, 6, D_MODEL)),
        "ffn_w_gu": w(ks[4], (DEPTH, 2, D_MODEL, 2 * D_FF), D_MODEL),
        "ffn_w_down": w(ks[5], (DEPTH, 2, D_FF, D_MODEL), D_FF),
        "w_o": w(ks[6], (DEPTH, MIX_WIDTH, D_MODEL), MIX_WIDTH),
        "mem_norm": gain(ks[7], (D_MODEL,)),
        "mem_w_kv": w(ks[8], (DEPTH, D_MODEL, 2 * MEM_WIDTH), D_MODEL),
        "ret_w_in": w(ks[9], (N_RET_LAYERS, D_MODEL, RET_IN), D_MODEL),
        "ret_log_decay": ret_log_decay,
        "ret_head_norm": gain(ks[11], (N_RET_LAYERS, RET_HEADS, RET_DV)),
        "mla_w_in": w(ks[12], (N_MLA_LAYERS, D_MODEL, MLA_IN), D_MODEL),
        "mla_q_norm": gain(ks[13], (N_MLA_LAYERS, MLA_Q_RANK)),
        "mla_kv_norm": gain(ks[14], (N_MLA_LAYERS, MLA_KV_RANK)),
        "mla_w_uq": w(ks[15], (N_MLA_LAYERS, MLA_Q_RANK, MLA_HEADS * (MLA_NOPE + MLA_ROPE)), MLA_Q_RANK),
        "mla_w_ukv": w(ks[16], (N_MLA_LAYERS, MLA_KV_RANK, MLA_HEADS * (MLA_NOPE + MLA_DV)), MLA_KV_RANK),
    }


def reference(x, mem, positions, norm_gains, ffn_w_gu, ffn_w_down, w_o, mem_norm, mem_w_kv,
              ret_w_in, ret_log_decay, ret_head_norm,
              mla_w_in, mla_q_norm, mla_kv_norm, mla_w_uq, mla_w_ukv):
    cos_ret, sin_ret = rope_tables(positions, RET_DK)
    cos_mla, sin_mla = rope_tables(positions, MLA_ROPE)
    mem_n = rms_norm(mem, mem_norm)

    for l in range(DEPTH):
        ng = norm_gains[l]
        x = x + FFN_RES_WEIGHT * rms_norm(swiglu(rms_norm(x, ng[0]), ffn_w_gu[l, 0], ffn_w_down[l, 0]), ng[1])
        h = rms_norm(x, ng[2])
        r = l // N_MIXERS
        if l % N_MIXERS == 0:
            tok, qm = retention_mixer(h, ret_w_in[r], ret_log_decay[r], ret_head_norm[r], cos_ret, sin_ret)
        else:
            tok, qm = mla_mixer(h, mla_w_in[r], mla_q_norm[r], mla_kv_norm[r], mla_w_uq[r], mla_w_ukv[r],
                                cos_mla, sin_mla)
        mem_out = memory_attention(qm, mem_n, mem_w_kv[l])
        mix = jnp.concatenate([tok, mem_out], axis=-1) @ w_o[l]
        x = x + rms_norm(mix, ng[3])
        x = x + FFN_RES_WEIGHT * rms_norm(swiglu(rms_norm(x, ng[4]), ffn_w_gu[l, 1], ffn_w_down[l, 1]), ng[5])
    return x
```

```python
from contextlib import ExitStack
import numpy as np
import concourse.bass as bass
import concourse.mybir as mybir
from concourse.bass_utils import run_bass_kernel_spmd

F32 = mybir.dt.float32
BF16 = mybir.dt.bfloat16
I32 = mybir.dt.int32
AF = mybir.ActivationFunctionType
ALU = mybir.AluOpType

D = 1024
S = 2048
DFF = 2816
NF = DFF // 128
DEPTH = 4
EPS = 1e-6
TG = 512
NBLK = S // 512


class Res:
    __slots__ = ("name", "w", "r")

    def __init__(self, name):
        self.name = name
        self.w = None
        self.r = []


class DmaSem:
    def __init__(self, sem):
        self.sem = sem
        self.total = 0


class Sched:
    ENG = ("pe", "act", "dve", "pool", "sp")

    def __init__(self, sems):
        self.sem = sems
        self.prog = {e: [] for e in self.ENG}
        self.nops = {e: 0 for e in self.ENG}
        self.waited = {e: {} for e in self.ENG}
        self.needed = set()

    def _wait(self, e, key, val):
        if self.waited[e].get(key, 0) >= val:
            return
        self.waited[e][key] = val
        if isinstance(key, DmaSem):
            self.prog[e].append(("wd", key, val))
        else:
            self.needed.add((key, val))
            self.prog[e].append(("w", key, val))

    def _deps(self, e, reads, writes):
        for r in reads:
            if r.w is not None:
                k, v = r.w
                if not (k == e and e == "pe"):
                    self._wait(e, k, v)
        for w in writes:
            if w.w is not None:
                k, v = w.w
                if not (k == e and e == "pe"):
                    self._wait(e, k, v)
            for (k, v) in w.r:
                if not (k == e and e == "pe"):
                    self._wait(e, k, v)

    def op(self, e, meth, reads=(), writes=(), **kw):
        self._deps(e, reads, writes)
        self.nops[e] += 1
        c = self.nops[e]
        self.prog[e].append(("op", (meth, kw), c))
        for w in writes:
            w.w = (e, c)
            w.r = []
        for r in reads:
            r.r.append((e, c))
        return c

    def dma(self, e, dsem, pairs, reads=(), writes=()):
        self._deps(e, reads, writes)
        for (o, i) in pairs:
            dsem.total += 16
            self.prog[e].append(("dma", (o, i), dsem))
        ev = (dsem, dsem.total)
        for w in writes:
            w.w = ev
            w.r = []
        for r in reads:
            r.r.append(ev)

    def wait_all(self, e, items):
        for r in items:
            if r.w is not None:
                self._wait(e, *r.w)
            for (k, v) in r.r:
                self._wait(e, k, v)

    def barrier(self, engs=("pe", "act", "dve", "pool")):
        for e in engs:
            for f in engs:
                if f != e and self.nops[f] > 0:
                    self._wait(e, f, self.nops[f])

    def replay(self, e, eng):
        cnt = 0
        num = {}
        for it in self.prog[e]:
            if it[0] == "op" and (e, it[2]) in self.needed:
                cnt += 1
                num[it[2]] = cnt
        self._num = getattr(self, "_num", {})
        self._num[e] = num

    def emit(self, e, eng):
        for it in self.prog[e]:
            k = it[0]
            if k == "w":
                _, key, val = it
                eng.wait_ge(self.sem[key], self._num[key][val])
            elif k == "wd":
                _, ds, val = it
                eng.wait_ge(ds.sem, val)
            elif k == "op":
                _, (meth, kw), c = it
                ins = getattr(eng, meth)(**kw)
                if (e, c) in self.needed:
                    ins.then_inc(self.sem[e], 1)
            elif k == "dma":
                _, (o, i), ds = it
                eng.dma_start(out=o, in_=i).then_inc(ds.sem, 16)


import math


class Arena:
    def __init__(self, t, nwords):
        self.t = t
        self.n = nwords
        self.off = 0
        self.peak = 0

    def reset(self, off=0):
        self.off = off

    def alloc(self, shape, dt):
        free = 1
        for d in shape[1:]:
            free *= d
        esz = 4 if dt in (F32, I32) else 2
        words = (free * esz + 3) // 4
        words = (words + 7) // 8 * 8
        assert self.off + words <= self.n, f"arena overflow: {self.off}+{words} > {self.n}"
        v = self.t[:, self.off:self.off + words]
        self.off += words
        self.peak = max(self.peak, self.off)
        if dt != F32:
            v = v.bitcast(dt)
        v = v[:, 0:free]
        if len(shape) == 3:
            v = v.rearrange("p (a b) -> p a b", a=shape[1])
        elif len(shape) == 4:
            v = v.rearrange("p (a b c) -> p a b c", a=shape[1], b=shape[2])
        return v


def build_program(dbg=None):
    dbg = dbg or {}
    n_layers = dbg.get("n_layers", DEPTH)
    nc = bass.Bass("TRN2", target_bir_lowering=False)

    used = {}
    NL = dbg.get("nl_w", DEPTH)
    SHAPES = {
        "x": ([S, D], F32), "mem": ([256, D], F32), "positions": ([1, S], I32),
        "norm_gains": ([DEPTH * 6, D], F32), "ffn_w_gu": ([NL, 2, D, 2 * DFF], F32),
        "ffn_w_down": ([NL, 2, DFF, D], F32), "w_o": ([NL, D, D], F32),
        "mem_norm": ([8, 128], F32), "mem_w_kv": ([NL, D, 512], F32),
        "ret_w_in": ([2, D, 3328], F32), "ret_log_decay": ([1, 24], F32),
        "ret_head_norm": ([12, 128], F32), "mla_w_in": ([2, D, 672], F32),
        "mla_q_norm": ([4, 128], F32), "mla_kv_norm": ([2, 128], F32),
        "mla_w_uq": ([2, 256, 576], F32), "mla_w_ukv": ([2, 128, 1152], F32),
        "c_ident": ([128, 128], F32), "c_cols": ([128, 8], F32), "c_tabs": ([128, 4, 128], F32),
    }

    class _Lazy:
        def __init__(self, name):
            self.name = name

        def get(self):
            if self.name not in used:
                shp, dt = SHAPES[self.name]
                used[self.name] = nc.dram_tensor(self.name, list(shp), dt, kind="ExternalInput").ap()
            return used[self.name]

        def __getitem__(self, k):
            return self.get()[k]

    x_d = _Lazy("x"); mem_d = _Lazy("mem"); pos_d = _Lazy("positions"); ng_d = _Lazy("norm_gains")
    wgu_d = _Lazy("ffn_w_gu"); wdn_d = _Lazy("ffn_w_down"); wo_d = _Lazy("w_o")
    memnorm_d = _Lazy("mem_norm"); memwkv_d = _Lazy("mem_w_kv"); retwin_d = _Lazy("ret_w_in")
    retld_d = _Lazy("ret_log_decay"); rethn_d = _Lazy("ret_head_norm"); mlawin_d = _Lazy("mla_w_in")
    mlaqn_d = _Lazy("mla_q_norm"); mlakvn_d = _Lazy("mla_kv_norm"); mlawuq_d = _Lazy("mla_w_uq")
    mlawukv_d = _Lazy("mla_w_ukv"); ident_d = _Lazy("c_ident"); ccols_d = _Lazy("c_cols"); ctabs_d = _Lazy("c_tabs")
    out_d = nc.dram_tensor("out", [S, D], F32, kind="ExternalOutput").ap()
    tabs_d = nc.dram_tensor("tabs_scratch", [4, 128, S], F32).ap()

    es = ExitStack()
    with es:
        def sb(name, shape, dt):
            return es.enter_context(nc.sbuf_tensor(name, list(shape), dt))

        def sem(name):
            return es.enter_context(nc.semaphore(name))

        sems = {e: sem("s_" + e) for e in Sched.ENG}
        sc = Sched(sems)
        op = sc.op

        xT = sb("xT", [128, 8, S], F32)
        xT_r = {(c, b): Res(f"xT{c}_{b}") for c in range(8) for b in range(NBLK)}
        NSLOT = 3
        SLOTE = 4096
        ring = sb("ring", [128, NSLOT, SLOTE], BF16)
        ring_r = [Res(f"ring{i}") for i in range(NSLOT)]
        ring_s = [DmaSem(sem(f"s_ring{i}")) for i in range(NSLOT)]
        ring_n = [0]
        ident = sb("ident", [128, 128], F32); ident_r = Res("ident")
        ident_bf = sb("ident_bf", [128, 128], BF16)
        ones_bf = sb("ones_bf", [128, 128], BF16); ones_r = Res("ones")
        gcol = sb("gcol", [128, 8, 24], F32)
        gcolh = sb("gcolh", [128, 8, 24], F32)
        gcol_r = Res("gcol")
        vcol = sb("vcol", [128, 26], F32); vcol_r = Res("vcol")
        ccols = sb("ccols", [128, 8], F32); ccols_r = Res("ccols")
        ctabs = sb("ctabs", [128, 4, 128], F32); ctabs_r = Res("ctabs")
        ldc = sb("ldc", [128, 24], F32); ldc_r = Res("ldc")
        eps_col = sb("eps_col", [128, 1], F32); eps_r = Res("eps")
        memT = sb("memT", [128, 8, 256], BF16); memT_r = Res("memT")
        all_dsems = []

        def one_sem():
            d = DmaSem(sem(f"s_one{len(all_dsems)}"))
            all_dsems.append(d)
            return d
        misc2_s = DmaSem(sem("s_misc2"))
        xst_s = [one_sem(), one_sem()]
        xsto_s = [one_sem(), one_sem()]
        kT_s = one_sem()
        tab_s = [DmaSem(sem(f"s_tab{i}")) for i in range(2)]
        out_s = DmaSem(sem("s_out"))
        ARENA_W = dbg.get("arena_words", 27400)
        arena_t = sb("arena", [128, ARENA_W], F32)
        AR = Arena(arena_t, ARENA_W)

        banks = [es.enter_context(nc.psum_tensor(f"bank{i}", [128, 512], F32)) for i in range(8)]
        bank_r = [Res(f"bank{i}") for i in range(8)]
        bank_n = [0]

        def next_bank():
            i = bank_n[0] % 8
            bank_n[0] += 1
            return banks[i], bank_r[i]

        def ring_load(pairs_fn):
            i = ring_n[0] % NSLOT
            ring_n[0] += 1
            sc.dma("pool", ring_s[i], pairs_fn(ring[:, i]), writes=[ring_r[i]])
            return ring[:, i], ring_r[i]

        def phase_barrier():
            for ds in [misc2_s, tab_s[0], tab_s[1]] + all_dsems:
                if ds.total > 0:
                    sc._wait("sp", ds, ds.total)
            op("sp", "nop", [], [])
            sc.barrier(("pe", "act", "dve", "sp"))

        sc.dma("sp", one_sem(), [(ident[:, :], ident_d[:, :]), (ccols[:, :], ccols_d[:, :]),
                              (ctabs[:, :, :], ctabs_d[:, :, :])], writes=[ident_r, ccols_r, ctabs_r])
        sc.dma("sp", one_sem(), [(ldc[:, :], retld_d[0:1, :].partition_broadcast(128))], writes=[ldc_r])
        op("dve", "memset", [], [ones_r], ap=ones_bf[:, :], constant=1.0)
        op("dve", "memset", [], [eps_r], ap=eps_col[:, :], constant=EPS)
        op("dve", "tensor_copy", [ident_r], [ones_r], out=ident_bf[:, :], in_=ident[:, :])

        AR.reset()
        gstage = AR.alloc([128, D], F32); gstage_r = Res("gstage")
        vstage = AR.alloc([128, 128], F32); vstage_r = Res("vstage")
        sc.dma("sp", one_sem(), [(gstage[0:24, :], ng_d[:, :])], writes=[gstage_r])
        sc.dma("sp", one_sem(), [(vstage[0:12, :], rethn_d[:, :]), (vstage[12:16, :], mlaqn_d[:, :]),
                              (vstage[16:18, :], mlakvn_d[:, :]), (vstage[18:26, :], memnorm_d[:, :])],
               writes=[vstage_r])
        for c in range(8):
            bk, bkr = next_bank()
            op("pe", "transpose", [gstage_r, ident_r], [bkr], out=bk[:, 0:24],
               in_=gstage[0:24, c * 128:(c + 1) * 128], identity=ident[0:24, 0:24])
            op("dve", "tensor_copy", [bkr], [gcol_r], out=gcol[:, c, :], in_=bk[:, 0:24])
            op("act", "mul", [bkr], [gcol_r], out=gcolh[:, c, :], in_=bk[:, 0:24], mul=0.5)
        bk, bkr = next_bank()
        op("pe", "transpose", [vstage_r, ident_r], [bkr], out=bk[:, 0:26], in_=vstage[0:26, :],
           identity=ident[0:26, 0:26])
        op("dve", "tensor_copy", [bkr], [vcol_r], out=vcol[:, :], in_=bk[:, 0:26])

        PI = math.pi
        TWO_PI = 2.0 * math.pi
        C1 = 6.28125
        C2 = TWO_PI - C1
        PI_LO = 3.1415925
        if dbg.get("tables", True):
            posi = AR.alloc([128, S], I32); posi_r = Res("posi")
            posf = AR.alloc([128, S], F32); posf_r = Res("posf")
            ang = AR.alloc([128, S], F32); ang_r = Res("ang")
            tv = AR.alloc([128, S], F32); tv_r = Res("tv")
            ki = AR.alloc([128, S], I32); ki_r = Res("ki")
            kf = AR.alloc([128, S], F32); kf_r = Res("kf")
            sc.dma("sp", one_sem(), [(posi[:, :], pos_d[0:1, :].partition_broadcast(128))], writes=[posi_r])
            op("dve", "tensor_copy", [posi_r], [posf_r], out=posf[:, :], in_=posi[:, :])
            ti = 0
            for kind in range(2):
                for cs in range(2):
                    if cs == 0:
                        op("dve", "tensor_scalar", [posf_r, ccols_r], [ang_r], out=ang[:, :], in0=posf[:, :],
                           scalar1=ccols[:, kind:kind + 1], scalar2=PI / 2, op0=ALU.mult, op1=ALU.add)
                    else:
                        op("dve", "tensor_scalar", [posf_r, ccols_r], [ang_r], out=ang[:, :], in0=posf[:, :],
                           scalar1=ccols[:, kind:kind + 1], scalar2=None, op0=ALU.mult)
                    op("dve", "tensor_scalar", [ang_r], [tv_r], out=tv[:, :], in0=ang[:, :],
                       scalar1=1.0 / TWO_PI, scalar2=0.5, op0=ALU.mult, op1=ALU.add)
                    op("dve", "tensor_copy", [tv_r], [ki_r], out=ki[:, :], in_=tv[:, :])
                    op("dve", "tensor_copy", [ki_r], [kf_r], out=kf[:, :], in_=ki[:, :])
                    op("dve", "scalar_tensor_tensor", [kf_r, ang_r], [tv_r], out=tv[:, :], in0=kf[:, :],
                       scalar=-C1, in1=ang[:, :], op0=ALU.mult, op1=ALU.add)
                    op("dve", "scalar_tensor_tensor", [kf_r, tv_r], [ang_r], out=ang[:, :], in0=kf[:, :],
                       scalar=-C2, in1=tv[:, :], op0=ALU.mult, op1=ALU.add)
                    op("dve", "tensor_scalar", [ang_r], [tv_r], out=tv[:, :], in0=ang[:, :],
                       scalar1=-PI, scalar2=TWO_PI, op0=ALU.is_lt, op1=ALU.mult)
                    op("dve", "tensor_tensor", [ang_r, tv_r], [ang_r], out=ang[:, :], in0=ang[:, :], in1=tv[:, :],
                       op=ALU.add)
                    op("dve", "tensor_scalar", [ang_r], [tv_r], out=tv[:, :], in0=ang[:, :],
                       scalar1=PI, scalar2=-TWO_PI, op0=ALU.is_gt, op1=ALU.mult)
                    op("dve", "tensor_tensor", [ang_r, tv_r], [ang_r], out=ang[:, :], in0=ang[:, :], in1=tv[:, :],
                       op=ALU.add)
                    op("dve", "tensor_scalar", [ang_r], [ang_r], out=ang[:, :], in0=ang[:, :],
                       scalar1=PI_LO, scalar2=-PI_LO, op0=ALU.min, op1=ALU.max)
                    if cs == 0:
                        op("act", "activation", [ang_r], [kf_r], out=kf[:, :], in_=ang[:, :], func=AF.Sin)
                    else:
                        op("act", "activation", [ang_r, ccols_r], [kf_r], out=kf[:, :], in_=ang[:, :], func=AF.Sin,
                           scale=ccols[:, 2 + kind:3 + kind])
                    sc.dma("sp", misc2_s, [(tabs_d[ti], kf[:, :])], reads=[kf_r])
                    ti += 1
            sc._wait("sp", misc2_s, misc2_s.total)

        if dbg.get("memprep", True):
            phase_barrier()
            AR.reset()
            for mt in range(2):
                mst = AR.alloc([128, D], F32); mst_r = Res("mst")
                msq = AR.alloc([128, D], F32); msq_r = Res("msq")
                mss = AR.alloc([128, 2], F32); mss_r = Res("mss")
                sc.dma("sp", one_sem(), [(mst[:, :], mem_d[mt * 128:(mt + 1) * 128, :])], writes=[mst_r])
                op("act", "activation", [mst_r], [msq_r, mss_r], out=msq[:, :], in_=mst[:, :], func=AF.Square,
                   accum_out=mss[:, 0:1])
                op("act", "activation", [mss_r, eps_r], [mss_r], out=mss[:, 1:2], in_=mss[:, 0:1], func=AF.Sqrt,
                   bias=eps_col[:, 0:1], scale=1.0 / D)
                op("dve", "reciprocal", [mss_r], [mss_r], out=mss[:, 0:1], in_=mss[:, 1:2])
                op("dve", "tensor_scalar", [mst_r, mss_r], [msq_r], out=msq[:, :], in0=mst[:, :],
                   scalar1=mss[:, 0:1], scalar2=None, op0=ALU.mult)
                for half in range(2):
                    bk, bkr = next_bank()
                    for j in range(4):
                        c = half * 4 + j
                        op("pe", "transpose", [msq_r, ident_r], [bkr], out=bk[:, j * 128:(j + 1) * 128],
                           in_=msq[:, c * 128:(c + 1) * 128], identity=ident[:, :])
                    for j in range(4):
                        c = half * 4 + j
                        op("dve", "tensor_scalar", [bkr, vcol_r], [memT_r], out=memT[:, c, mt * 128:(mt + 1) * 128],
                           in0=bk[:, j * 128:(j + 1) * 128], scalar1=vcol[:, 18 + c:19 + c], scalar2=None,
                           op0=ALU.mult)

        phase_barrier()
        AR.reset()
        xstage = [AR.alloc([128, D], F32) for _ in range(2)]
        xstage_r = [Res(f"xst{i}") for i in range(2)]
        for t in range(16):
            si = t % 2
            sc.dma("sp", xst_s[si], [(xstage[si][:, :], x_d[t * 128:(t + 1) * 128, :])], writes=[xstage_r[si]])
            for half in range(2):
                bk, bkr = next_bank()
                for j in range(4):
                    c = half * 4 + j
                    op("pe", "transpose", [xstage_r[si], ident_r], [bkr], out=bk[:, j * 128:(j + 1) * 128],
                       in_=xstage[si][:, c * 128:(c + 1) * 128], identity=ident[:, :])
                wr = [xT_r[(half * 4 + j, t // 4)] for j in range(4)]
                o_ap = xT[:, half * 4:half * 4 + 4, t * 128:(t + 1) * 128]
                i_ap = bk[:, :].rearrange("p (j q) -> p j q", j=4)
                if half == 0:
                    op("act", "copy", [bkr], wr, out=o_ap, in_=i_ap)
                else:
                    op("dve", "tensor_copy", [bkr], wr, out=o_ap, in_=i_ap)

        def rstd_from_sq(sq_ap, sq_res, nchunks, inv_n, rtmp, rtmp_r, rs, rs_r, np_=128, lnexp=True):
            bk, bkr = next_bank()
            for c in range(nchunks):
                op("pe", "matmul", [ones_r] + sq_res, [bkr], out=bk[:, :], lhsT=ones_bf[:, :], rhs=sq_ap[:, c, :],
                   start=(c == 0), stop=(c == nchunks - 1))
            if lnexp:
                op("act", "activation", [bkr, eps_r], [rtmp_r], out=rtmp[:, :], in_=bk[:, :], func=AF.Ln,
                   bias=eps_col[:, 0:1], scale=inv_n)
                op("act", "activation", [rtmp_r], [rs_r], out=rs[:, :], in_=rtmp[:, :], func=AF.Exp, scale=-0.5)
            else:
                op("act", "activation", [bkr, eps_r], [rtmp_r], out=rtmp[:, :], in_=bk[:, :], func=AF.Sqrt,
                   bias=eps_col[:, 0:1], scale=inv_n)
                op("dve", "reciprocal", [rtmp_r], [rs_r], out=rs[:, :], in_=rtmp[:, :])

        class PostNorm:
            def __init__(self, lnexp=True):
                self.lnexp = lnexp
                self.yT = AR.alloc([128, 8, 512], F32); self.yT_r = [Res(f"yT{c}") for c in range(8)]
                self.sqb = AR.alloc([128, 8, 512], BF16); self.sqb_r = [Res(f"sqb{c}") for c in range(8)]
                self.rtmp = AR.alloc([128, 512], F32); self.rtmp_r = Res("rtmp")
                self.rs = AR.alloc([128, 512], F32); self.rs_r = Res("rs_post")
                self.upd = [AR.alloc([128, 512], F32) for _ in range(2)]
                self.upd_r = [Res(f"upd{i}") for i in range(2)]
                self.n = 0

            def take(self, c, bk, bkr):
                op("act", "copy", [bkr], [self.yT_r[c]], out=self.yT[:, c, :], in_=bk[:, :])
                if self.lnexp:
                    op("act", "activation", [bkr], [self.sqb_r[c]], out=self.sqb[:, c, :], in_=bk[:, :], func=AF.Square)
                else:
                    op("pool", "tensor_tensor", [self.yT_r[c]], [self.sqb_r[c]], out=self.sqb[:, c, :],
                       in0=self.yT[:, c, :], in1=self.yT[:, c, :], op=ALU.mult)

            def finish(self, blk, gi, half):
                tsl = slice(blk * 512, (blk + 1) * 512)
                rstd_from_sq(self.sqb, self.sqb_r, 8, 1.0 / D, self.rtmp, self.rtmp_r, self.rs, self.rs_r,
                             lnexp=self.lnexp)
                gc = gcolh if half else gcol
                for c in range(8):
                    ui = self.n % 2
                    self.n += 1
                    xr = xT_r[(c, blk)]
                    op("dve", "tensor_tensor", [self.yT_r[c], self.rs_r], [self.upd_r[ui]],
                       out=self.upd[ui][:, :], in0=self.yT[:, c, :], in1=self.rs[:, :], op=ALU.mult)
                    op("dve", "scalar_tensor_tensor", [self.upd_r[ui], gcol_r, xr], [xr], out=xT[:, c, tsl],
                       in0=self.upd[ui][:, :], scalar=gc[:, c, gi:gi + 1], in1=xT[:, c, tsl],
                       op0=ALU.mult, op1=ALU.add)

        class PreNorm:
            def __init__(self, lnexp=True):
                self.lnexp = lnexp
                self.sqa = AR.alloc([128, 8, 512], BF16); self.sqa_r = Res("sqa")
                self.rtmp = AR.alloc([128, 512], F32); self.rtmp_r = Res("rtmp_pre")
                self.rs = AR.alloc([128, 512], F32); self.rs_r = Res("rs_pre")

            def run(self, blk, gi, out_ap, out_res):
                tsl = slice(blk * 512, (blk + 1) * 512)
                xr = [xT_r[(c, blk)] for c in range(8)]
                op("act", "activation", xr, [self.sqa_r], out=self.sqa[:, :, :], in_=xT[:, :, tsl], func=AF.Square)
                rstd_from_sq(self.sqa, [self.sqa_r], 8, 1.0 / D, self.rtmp, self.rtmp_r, self.rs, self.rs_r,
                             lnexp=self.lnexp)
                for c in range(8):
                    op("dve", "scalar_tensor_tensor", [xr[c], gcol_r, self.rs_r], [out_res],
                       out=out_ap[:, c, :], in0=xT[:, c, tsl], scalar=gcol[:, c, gi:gi + 1],
                       in1=self.rs[:, :], op0=ALU.mult, op1=ALU.mult)

        def ffn(l, j):
            phase_barrier()
            AR.reset()
            gi_pre = l * 6 + (0 if j == 0 else 4)
            gi_post = l * 6 + (1 if j == 0 else 5)
            wgu = wgu_d[l, j].rearrange("(k p) f -> p k f", p=128)
            wdn = wdn_d[l, j].rearrange("(fc p) d -> p fc d", p=128)
            pre = PreNorm()
            post = PostNorm()
            hTg = [AR.alloc([128, 8, 512], BF16) for _ in range(2)]
            hTg_r = [Res(f"hTg{i}") for i in range(2)]
            actT = AR.alloc([128, NF, 512], BF16)
            actT_r = [Res(f"actT{f}") for f in range(NF)]
            silu_t = [AR.alloc([128, 512], BF16) for _ in range(2)]
            silu_r = [Res(f"silu{i}") for i in range(2)]
            nsl = 0
            pre.run(0, gi_pre, hTg[0], hTg_r[0])
            for g in range(NBLK):
                blk = g
                hb = g % 2
                tsl = slice(blk * 512, (blk + 1) * 512)
                if dbg.get("ffn_stage", 3) == 1:
                    for c in range(8):
                        op("act", "copy", [hTg_r[hb]], [xT_r[(c, blk)]], out=xT[:, c, tsl], in_=hTg[hb][:, c, :])
                    continue
                for fp in range(NF // 2):
                    def pairs(slot_ap, fp=fp):
                        v = slot_ap.rearrange("p (k two f) -> p k two f", k=8, two=2)
                        return [(v[:, :, 0, :], wgu[:, :, fp * 256:(fp + 1) * 256]),
                                (v[:, :, 1, :], wgu[:, :, DFF + fp * 256:DFF + (fp + 1) * 256])]
                    wslot, wres = ring_load(pairs)
                    wv = wslot.rearrange("p (k two f) -> p k two f", k=8, two=2)
                    for ff in range(2):
                        f = fp * 2 + ff
                        bg, bgr = next_bank()
                        bu, bur = next_bank()
                        for (which, bk, bkr) in ((0, bg, bgr), (1, bu, bur)):
                            for k in range(8):
                                op("pe", "matmul", [wres, hTg_r[hb]], [bkr], out=bk[:, :],
                                   lhsT=wv[:, k, which, ff * 128:(ff + 1) * 128], rhs=hTg[hb][:, k, :],
                                   start=(k == 0), stop=(k == 7))
                        si = nsl % 2
                        nsl += 1
                        op("act", "activation", [bgr], [silu_r[si]], out=silu_t[si][:, :], in_=bg[:, :], func=AF.Silu)
                        op("dve", "tensor_tensor", [bur, silu_r[si]], [actT_r[f]], out=actT[:, f, :], in0=bu[:, :],
                           in1=silu_t[si][:, :], op=ALU.mult)
                if dbg.get("ffn_stage", 3) == 2:
                    for c in range(8):
                        op("act", "copy", [actT_r[c]], [xT_r[(c, blk)]], out=xT[:, c, tsl], in_=actT[:, c, :])
                    continue
                if g + 1 < NBLK:
                    pre.run(g + 1, gi_pre, hTg[(g + 1) % 2], hTg_r[(g + 1) % 2])
                for dh in range(2):
                    bks = [next_bank() for _ in range(4)]
                    f0 = 0
                    while f0 < NF:
                        nfc = min(8, NF - f0)

                        def pairs(slot_ap, f0=f0, nfc=nfc, dh=dh):
                            v = slot_ap.rearrange("p (fc d) -> p fc d", d=512)
                            return [(v[:, 0:nfc, :], wdn[:, f0:f0 + nfc, dh * 512:(dh + 1) * 512])]
                        wslot, wres = ring_load(pairs)
                        wv = wslot.rearrange("p (fc d) -> p fc d", d=512)
                        for fi in range(nfc):
                            f = f0 + fi
                            for dc in range(4):
                                bk, bkr = bks[dc]
                                op("pe", "matmul", [wres, actT_r[f]], [bkr], out=bk[:, :],
                                   lhsT=wv[:, fi, dc * 128:(dc + 1) * 128], rhs=actT[:, f, :],
                                   start=(f == 0), stop=(f == NF - 1))
                        f0 += nfc
                    for dc in range(4):
                        bk, bkr = bks[dc]
                        post.take(dh * 4 + dc, bk, bkr)
                if dbg.get("ffn_stage", 3) == 25:
                    for c in range(8):
                        op("act", "copy", [post.yT_r[c]], [xT_r[(c, blk)]], out=xT[:, c, tsl], in_=post.yT[:, c, :])
                    continue
                post.finish(blk, gi_post, True)

        def load_tab_window(tw, tw_r, which, blk, k):
            sc.dma("sp", tab_s[k], [(tw[k][:, 0, :], tabs_d[2 * which, :, blk * 512:(blk + 1) * 512]),
                                    (tw[k][:, 1, :], tabs_d[2 * which + 1, :, blk * 512:(blk + 1) * 512])],
                   writes=[tw_r[k]])

        def mem_attention(l, hT, hT_r, wqm_fn, memoT, memoT_r):
            mkT = AR.alloc([128, 4, 256], BF16); mkT_r = Res("mkT")
            mv = AR.alloc([128, 2, 256], BF16); mv_r = Res("mv")
            qmsb = [AR.alloc([128, 512], BF16) for _ in range(2)]; qmsb_r = [Res(f"qmsb{i}") for i in range(2)]
            pT = [AR.alloc([128, 2, 512], BF16) for _ in range(2)]; pT_r = [Res(f"mpT{i}") for i in range(2)]
            rden = AR.alloc([128, 512], F32); rden_r = Res("mrden")
            lden = AR.alloc([128, 512], F32); lden_r = Res("mlden")

            def pairs_kv(slot_ap):
                v = slot_ap.rearrange("p (k f) -> p k f", k=8)
                return [(v[:, :, :], memwkv_d[l].rearrange("(k p) f -> p k f", p=128))]
            wkv, wkv_r = ring_load(pairs_kv)
            wkv = wkv.rearrange("p (k f) -> p k f", k=8)
            wqm, wqm_r = ring_load(wqm_fn)
            wqm = wqm[:, 0:2048].rearrange("p (k f) -> p k f", k=8)
            for a in range(4):
                bk, bkr = next_bank()
                for k in range(8):
                    op("pe", "matmul", [wkv_r, memT_r], [bkr], out=bk[0:64, 0:256], lhsT=wkv[:, k, a * 64:(a + 1) * 64],
                       rhs=memT[:, k, :], start=(k == 0), stop=(k == 7))
                op("act", "copy", [bkr], [mkT_r], out=mkT[0:64, a, :], in_=bk[0:64, 0:256])
            for mt in range(2):
                bk, bkr = next_bank()
                for k in range(8):
                    op("pe", "matmul", [wkv_r, memT_r], [bkr], out=bk[:, 0:256], lhsT=memT[:, k, mt * 128:(mt + 1) * 128],
                       rhs=wkv[:, k, 256:512], start=(k == 0), stop=(k == 7))
                op("act", "copy", [bkr], [mv_r], out=mv[:, mt, :], in_=bk[:, 0:256])
            items = [(b, a) for b in range(NBLK) for a in range(4)]
            sbank = {}

            def st1(i):
                b, a = items[i]
                tsl = slice(b * 512, (b + 1) * 512)
                bq, bqr = next_bank()
                for k in range(8):
                    op("pe", "matmul", [wqm_r, hT_r], [bqr], out=bq[0:64, :], lhsT=wqm[:, k, a * 64:(a + 1) * 64],
                       rhs=hT[:, k, tsl], start=(k == 0), stop=(k == 7))
                op("act", "copy", [bqr], [qmsb_r[i % 2]], out=qmsb[i % 2][0:64, :], in_=bq[0:64, :])

            def st2(i):
                b, a = items[i]
                for mt in range(2):
                    bs, bsr = next_bank()
                    op("pe", "matmul", [mkT_r, qmsb_r[i % 2]], [bsr], out=bs[:, :],
                       lhsT=mkT[0:64, a, mt * 128:(mt + 1) * 128], rhs=qmsb[i % 2][0:64, :], start=True, stop=True)
                    op("act", "activation", [bsr], [pT_r[i % 2]], out=pT[i % 2][:, mt, :], in_=bs[:, :], func=AF.Exp,
                       scale=0.125)

            def st3(i):
                b, a = items[i]
                tsl = slice(b * 512, (b + 1) * 512)
                i2 = i % 2
                bn, bnr = next_bank()
                bd, bdr = next_bank()
                for mt in range(2):
                    op("pe", "matmul", [mv_r, pT_r[i2]], [bnr], out=bn[0:64, :], lhsT=mv[:, mt, a * 64:(a + 1) * 64],
                       rhs=pT[i2][:, mt, :], start=(mt == 0), stop=(mt == 1))
                for mt in range(2):
                    op("pe", "matmul", [ones_r, pT_r[i2]], [bdr], out=bd[0:64, :], lhsT=ones_bf[:, 0:64],
                       rhs=pT[i2][:, mt, :], start=(mt == 0), stop=(mt == 1))
                op("act", "activation", [bdr], [lden_r], out=lden[0:64, :], in_=bd[0:64, :], func=AF.Ln)
                op("act", "activation", [lden_r], [rden_r], out=rden[0:64, :], in_=lden[0:64, :], func=AF.Exp, scale=-1.0)
                op("dve", "tensor_tensor", [bnr, rden_r], [memoT_r], out=memoT[0:64, a, tsl], in0=bn[0:64, :],
                   in1=rden[0:64, :], op=ALU.mult)
            n = len(items)
            for step in range(n + 2):
                if step < n:
                    st1(step)
                if 0 <= step - 1 < n:
                    st2(step - 1)
                if 0 <= step - 2 < n:
                    st3(step - 2)

        def out_proj(l, tokT, tokT_r, memoT, memoT_r, post):
            wo = wo_d[l]
            def p1(slot_ap):
                v = slot_ap.rearrange("p (h d) -> p h d", h=4)
                return [(v[:, :, :], wo[0:512, :].rearrange("(h p) d -> p h d", p=128))]
            def p2(slot_ap):
                v = slot_ap.rearrange("p (h d) -> p h d", h=4)
                return [(v[:, 0:2, :], wo[512:768, :].rearrange("(h p) d -> p h d", p=128)),
                        (v[0:64, 2:4, :], wo[768:896, :].rearrange("(a p) d -> p a d", p=64))]
            def p3(slot_ap):
                v = slot_ap.rearrange("p (h d) -> p h d", h=4)
                return [(v[0:64, 0:2, :], wo[896:1024, :].rearrange("(a p) d -> p a d", p=64))]
            s1, s1r = ring_load(p1)
            s2, s2r = ring_load(p2)
            s3, s3r = ring_load(p3)
            v1 = s1.rearrange("p (h d) -> p h d", h=4)
            v2 = s2.rearrange("p (h d) -> p h d", h=4)
            v3 = s3.rearrange("p (h d) -> p h d", h=4)
            gi = l * 6 + 3
            for b in range(NBLK):
                tsl = slice(b * 512, (b + 1) * 512)
                for c in range(8):
                    dsl = slice(c * 128, (c + 1) * 128)
                    bk, bkr = next_bank()
                    for h in range(6):
                        wv, wr = (v1[:, h, dsl], s1r) if h < 4 else (v2[:, h - 4, dsl], s2r)
                        op("pe", "matmul", [wr, tokT_r[h]], [bkr], out=bk[:, :], lhsT=wv, rhs=tokT[:, h, tsl],
                           start=(h == 0), stop=False)
                    for a in range(4):
                        wv, wr = (v2[0:64, 2 + a, dsl], s2r) if a < 2 else (v3[0:64, a - 2, dsl], s3r)
                        op("pe", "matmul", [wr, memoT_r], [bkr], out=bk[:, :], lhsT=wv, rhs=memoT[0:64, a, tsl],
                           start=False, stop=(a == 3))
                    post.take(c, bk, bkr)
                post.finish(b, gi, False)

        def retention_layer(l):
            r = l // 2
            phase_barrier()
            AR.reset()
            hT = AR.alloc([128, 8, S], BF16); hT_r = Res("hT")
            tokT = AR.alloc([128, 6, S], BF16); tokT_r = [Res(f"tokT{h}") for h in range(6)]
            base_off = AR.off
            pre = PreNorm()
            for b in range(NBLK):
                pre.run(b, l * 6 + 2, hT[:, :, b * 512:(b + 1) * 512], hT_r)
            phase_barrier()
            AR.reset(base_off)
            if dbg.get("ret_stop", 99) == 1:
                return
            win = retwin_d[r].rearrange("(k p) f -> p k f", p=128)
            qT = AR.alloc([128, S], BF16); qT_r = Res("qT")
            kT = AR.alloc([128, S], BF16); kT_r = Res("kT")
            kwf = AR.alloc([128, 16, 128], BF16); kwf_r = Res("kwf")
            kwb = AR.alloc([128, 16, 128], BF16); kwb_r = Res("kwb")
            Sf = AR.alloc([128, 16, 128], BF16); Sf_r = Res("Sf")
            Sb = AR.alloc([128, 16, 128], BF16); Sb_r = Res("Sb")
            vh = AR.alloc([128, 16, 128], BF16); vh_r = Res("vh")
            tw = [AR.alloc([128, 2, 512], F32) for _ in range(2)]; tw_r = [Res(f"tw{i}") for i in range(2)]
            t1 = AR.alloc([128, 512], F32); t1_r = Res("t1")
            t2 = AR.alloc([128, 512], F32); t2_r = Res("t2")
            Mt = AR.alloc([128, 128], F32); Mt_r = Res("Mt")
            marg = AR.alloc([128, 128], F32); marg_r = Res("marg")
            wq = AR.alloc([128, 2, 128], F32); wq_r = Res("wq")
            hc = AR.alloc([128, 8], F32); hc_r = Res("hc")
            Sst = AR.alloc([128, 2, 128], F32); Sst_r = [Res("Sstf"), Res("Sstb")]
            pT = AR.alloc([128, 512], BF16); pT_r = Res("rpT")
            qw_f32 = AR.alloc([128, 512], F32); qw_r = Res("qw")
            qw = qw_f32.bitcast(BF16).rearrange("p (d q) -> p d q", d=2)
            A, A_r = t1, t1_r
            B, B_r = t2, t2_r
            ybf = AR.alloc([128, 512], BF16); ybf_r = Res("ybf")
            ysq = AR.alloc([128, 512], BF16); ysq_r = Res("ysq")
            C, C_r = qw_f32, qw_r
            G = AR.alloc([128, 512], BF16); G_r = Res("G")
            op("dve", "memset", [], [Sf_r], ap=Sf[:, 0, :], constant=0.0)
            op("dve", "memset", [], [Sb_r], ap=Sb[:, 15, :], constant=0.0)
            SC = 128.0 ** -0.5
            ntw = 0
            for h in range(dbg.get("ret_heads", 6)):
                lf = ldc[:, r * 12 + h:r * 12 + h + 1]
                lb = ldc[:, r * 12 + 6 + h:r * 12 + 6 + h + 1]
                op("act", "activation", [ccols_r, ldc_r], [hc_r], out=hc[:, 0:1], in_=ccols[:, 4:5], func=AF.Exp, scale=lf)
                op("act", "activation", [ccols_r, ldc_r], [hc_r], out=hc[:, 1:2], in_=ccols[:, 5:6], func=AF.Exp, scale=lb)
                op("act", "activation", [ldc_r], [hc_r], out=hc[:, 2:3], in_=lf, func=AF.Exp, scale=128.0)
                op("act", "activation", [ldc_r], [hc_r], out=hc[:, 3:4], in_=lb, func=AF.Exp, scale=128.0)
                op("dve", "tensor_scalar", [hc_r], [hc_r], out=hc[:, 0:2], in0=hc[:, 0:2], scalar1=SC, scalar2=None,
                   op0=ALU.mult)
                op("act", "activation", [ctabs_r, ldc_r], [wq_r], out=wq[:, 0, :], in_=ctabs[:, 0, :], func=AF.Exp, scale=lf)
                op("act", "activation", [ctabs_r, ldc_r], [wq_r], out=wq[:, 1, :], in_=ctabs[:, 1, :], func=AF.Exp, scale=lb)
                op("dve", "tensor_scalar", [ctabs_r, ldc_r], [marg_r], out=marg[:, :], in0=ctabs[:, 3, :], scalar1=lb,
                   scalar2=None, op0=ALU.mult)
                op("dve", "scalar_tensor_tensor", [ctabs_r, ldc_r, marg_r], [marg_r], out=marg[:, :], in0=ctabs[:, 2, :],
                   scalar=lf, in1=marg[:, :], op0=ALU.mult, op1=ALU.add)
                op("act", "activation", [marg_r], [Mt_r], out=Mt[:, :], in_=marg[:, :], func=AF.Exp)
                op("dve", "tensor_scalar", [Mt_r], [Mt_r], out=Mt[:, :], in0=Mt[:, :], scalar1=SC, scalar2=None,
                   op0=ALU.mult)
                if dbg.get("ret_stop", 99) == 2:
                    return
                def pA(slot_ap, h=h):
                    v = slot_ap.rearrange("p (k f) -> p k f", k=8)
                    qo = h * 128
                    ko = 768 + h * 128
                    return [(v[:, :, 0:128], win[:, :, qo:qo + 128]),
                            (v[:, :, 128:192], win[:, :, qo + 64:qo + 128]), (v[:, :, 192:256], win[:, :, qo:qo + 64]),
                            (v[:, :, 256:384], win[:, :, ko:ko + 128]),
                            (v[:, :, 384:448], win[:, :, ko + 64:ko + 128]), (v[:, :, 448:512], win[:, :, ko:ko + 64])]
                def pB(slot_ap, h=h):
                    v = slot_ap[:, 0:2048].rearrange("p (k f) -> p k f", k=8)
                    return [(v[:, :, 0:128], win[:, :, 2304 + h * 128:2304 + (h + 1) * 128]),
                            (v[:, :, 128:256], win[:, :, 1536 + h * 128:1536 + (h + 1) * 128])]
                wA, wA_r = ring_load(pA)
                wA = wA.rearrange("p (k f) -> p k f", k=8)
                wB, wB_r = ring_load(pB)
                wB = wB[:, 0:2048].rearrange("p (k f) -> p k f", k=8)
                for b in range(NBLK):
                    tsl = slice(b * 512, (b + 1) * 512)
                    k2 = ntw % 2
                    ntw += 1
                    load_tab_window(tw, tw_r, 0, b, k2)
                    for (dst, dst_r, off) in ((qT, qT_r, 0), (kT, kT_r, 256)):
                        b0, b0r = next_bank()
                        b1, b1r = next_bank()
                        for (bk, bkr, o2) in ((b0, b0r, off), (b1, b1r, off + 128)):
                            for k in range(8):
                                op("pe", "matmul", [wA_r, hT_r], [bkr], out=bk[:, :], lhsT=wA[:, k, o2:o2 + 128],
                                   rhs=hT[:, k, tsl], start=(k == 0), stop=(k == 7))
                        op("dve", "tensor_tensor", [b0r, tw_r[k2]], [t1_r], out=t1[:, :], in0=b0[:, :], in1=tw[k2][:, 0, :],
                           op=ALU.mult)
                        op("dve", "tensor_tensor", [b1r, tw_r[k2]], [t2_r], out=t2[:, :], in0=b1[:, :], in1=tw[k2][:, 1, :],
                           op=ALU.mult)
                        op("dve", "tensor_tensor", [t1_r, t2_r], [dst_r], out=dst[:, tsl], in0=t1[:, :], in1=t2[:, :],
                           op=ALU.add)
                if dbg.get("ret_stop", 99) == 3:
                    return
                for tq in range(4):
                    bk, bkr = next_bank()
                    for j in range(4):
                        t = tq * 4 + j
                        for k in range(8):
                            op("pe", "matmul", [wB_r, hT_r], [bkr], out=bk[:, j * 128:(j + 1) * 128],
                               lhsT=hT[:, k, t * 128:(t + 1) * 128], rhs=wB[:, k, 128:256], start=(k == 0), stop=(k == 7))
                    op("act", "copy", [bkr], [vh_r], out=vh[:, tq * 4:tq * 4 + 4, :],
                       in_=bk[:, :].rearrange("p (j q) -> p j q", j=4))
                if dbg.get("ret_stop", 99) == 4:
                    return
                for tq in range(4):
                    bk, bkr = next_bank()
                    for j in range(4):
                        n = tq * 4 + j
                        op("pe", "matmul", [kT_r, ones_r], [bkr], out=bk[:, j * 128:(j + 1) * 128],
                           lhsT=kT[:, n * 128:(n + 1) * 128], rhs=ident_bf[:, :], start=True, stop=True)
                    bv = bk[:, :].rearrange("p (j q) -> p j q", j=4)
                    op("dve", "tensor_scalar", [bkr, hc_r], [kwf_r], out=kwf[:, tq * 4:tq * 4 + 4, :], in0=bv,
                       scalar1=hc[:, 0:1], scalar2=None, op0=ALU.mult)
                    op("dve", "tensor_scalar", [bkr, hc_r], [kwb_r], out=kwb[:, tq * 4:tq * 4 + 4, :], in0=bv,
                       scalar1=hc[:, 1:2], scalar2=None, op0=ALU.mult)
                if dbg.get("ret_stop", 99) == 5:
                    return
                fb = [next_bank() for _ in range(4)]
                for n in range(15):
                    bk, bkr = fb[n // 4]
                    op("pe", "matmul", [kwf_r, vh_r], [bkr], out=bk[:, (n % 4) * 128:(n % 4 + 1) * 128],
                       lhsT=kwf[:, n, :], rhs=vh[:, n, :], start=True, stop=True)
                bb = [next_bank() for _ in range(4)]
                for n in range(15, 0, -1):
                    bk, bkr = bb[n // 4]
                    op("pe", "matmul", [kwb_r, vh_r], [bkr], out=bk[:, (n % 4) * 128:(n % 4 + 1) * 128],
                       lhsT=kwb[:, n, :], rhs=vh[:, n, :], start=True, stop=True)
                for n in range(15):
                    bk, bkr = fb[n // 4]
                    kv = bk[:, (n % 4) * 128:(n % 4 + 1) * 128]
                    if n == 0:
                        op("dve", "tensor_copy", [bkr], [Sst_r[0]], out=Sst[:, 0, :], in_=kv)
                    else:
                        op("dve", "scalar_tensor_tensor", [bkr, hc_r, Sst_r[0]], [Sst_r[0]], out=Sst[:, 0, :],
                           in0=Sst[:, 0, :], scalar=hc[:, 2:3], in1=kv, op0=ALU.mult, op1=ALU.add)
                    op("act", "copy", [Sst_r[0]], [Sf_r], out=Sf[:, n + 1, :], in_=Sst[:, 0, :])
                for n in range(15, 0, -1):
                    bk, bkr = bb[n // 4]
                    kv = bk[:, (n % 4) * 128:(n % 4 + 1) * 128]
                    if n == 15:
                        op("dve", "tensor_copy", [bkr], [Sst_r[1]], out=Sst[:, 1, :], in_=kv)
                    else:
                        op("dve", "scalar_tensor_tensor", [bkr, hc_r, Sst_r[1]], [Sst_r[1]], out=Sst[:, 1, :],
                           in0=Sst[:, 1, :], scalar=hc[:, 3:4], in1=kv, op0=ALU.mult, op1=ALU.add)
                    op("act", "copy", [Sst_r[1]], [Sb_r], out=Sb[:, n - 1, :], in_=Sst[:, 1, :])
                if dbg.get("ret_stop", 99) == 6:
                    return
                for b in range(NBLK):
                    tsl = slice(b * 512, (b + 1) * 512)
                    bs, bsr = next_bank()
                    for j in range(4):
                        n = b * 4 + j
                        csl = slice(n * 128, (n + 1) * 128)
                        op("pe", "matmul", [kT_r, qT_r], [bsr], out=bs[:, j * 128:(j + 1) * 128], lhsT=kT[:, csl],
                           rhs=qT[:, csl], start=True, stop=True)
                    op("dve", "tensor_tensor", [bsr, Mt_r], [pT_r], out=pT[:, :].rearrange("p (j q) -> p j q", j=4),
                       in0=bs[:, :].rearrange("p (j q) -> p j q", j=4),
                       in1=Mt[:, :].unsqueeze(1).to_broadcast([128, 4, 128]), op=ALU.mult)
                    for d in range(2):
                        op("dve", "tensor_tensor", [qT_r, wq_r], [qw_r],
                           out=qw[:, d, :].rearrange("p (j q) -> p j q", j=4),
                           in0=qT[:, tsl].rearrange("p (j q) -> p j q", j=4),
                           in1=wq[:, d, :].unsqueeze(1).to_broadcast([128, 4, 128]), op=ALU.mult)
                    bg, bgr = next_bank()
                    for k in range(8):
                        op("pe", "matmul", [wB_r, hT_r], [bgr], out=bg[:, :], lhsT=wB[:, k, 0:128], rhs=hT[:, k, tsl],
                           start=(k == 0), stop=(k == 7))
                    op("act", "activation", [bgr], [G_r], out=G[:, :], in_=bg[:, :], func=AF.Silu)
                    by, byr = next_bank()
                    for j in range(4):
                        n = b * 4 + j
                        jsl = slice(j * 128, (j + 1) * 128)
                        op("pe", "matmul", [vh_r, pT_r], [byr], out=by[:, jsl], lhsT=vh[:, n, :], rhs=pT[:, jsl],
                           start=True, stop=False)
                        op("pe", "matmul", [Sf_r, qw_r], [byr], out=by[:, jsl], lhsT=Sf[:, n, :], rhs=qw[:, 0, jsl],
                           start=False, stop=False)
                        op("pe", "matmul", [Sb_r, qw_r], [byr], out=by[:, jsl], lhsT=Sb[:, n, :], rhs=qw[:, 1, jsl],
                           start=False, stop=True)
                    op("act", "copy", [byr], [A_r], out=A[:, :], in_=by[:, :])
                    op("act", "copy", [byr], [ybf_r], out=ybf[:, :], in_=by[:, :])
                    op("act", "activation", [byr], [ysq_r], out=ysq[:, :], in_=by[:, :], func=AF.Square)
                    bm, bmr = next_bank()
                    bq, bqr = next_bank()
                    op("pe", "matmul", [ones_r, ybf_r], [bmr], out=bm[:, :], lhsT=ones_bf[:, :], rhs=ybf[:, :],
                       start=True, stop=True)
                    op("pe", "matmul", [ones_r, ysq_r], [bqr], out=bq[:, :], lhsT=ones_bf[:, :], rhs=ysq[:, :],
                       start=True, stop=True)
                    op("act", "mul", [bmr], [B_r], out=B[:, :], in_=bm[:, :], mul=1.0 / 128)
                    op("dve", "tensor_tensor", [A_r, B_r], [C_r], out=C[:, :], in0=A[:, :], in1=B[:, :],
                       op=ALU.subtract)
                    op("act", "activation", [B_r], [A_r], out=A[:, :], in_=B[:, :], func=AF.Square)
                    op("dve", "scalar_tensor_tensor", [bqr, A_r], [B_r], out=B[:, :], in0=bq[:, :], scalar=1.0 / 128,
                       in1=A[:, :], op0=ALU.mult, op1=ALU.subtract)
                    op("act", "activation", [B_r, eps_r], [A_r], out=A[:, :], in_=B[:, :], func=AF.Ln,
                       bias=eps_col[:, 0:1], scale=1.0)
                    op("act", "activation", [A_r], [B_r], out=B[:, :], in_=A[:, :], func=AF.Exp, scale=-0.5)
                    op("dve", "scalar_tensor_tensor", [C_r, vcol_r, B_r], [A_r], out=A[:, :], in0=C[:, :],
                       scalar=vcol[:, r * 6 + h:r * 6 + h + 1], in1=B[:, :], op0=ALU.mult, op1=ALU.mult)
                    op("dve", "tensor_tensor", [A_r, G_r], [tokT_r[h]], out=tokT[:, h, tsl], in0=A[:, :], in1=G[:, :],
                       op=ALU.mult)
            if dbg.get("ret_stop", 99) == 8:
                return
            phase_barrier()
            AR.reset(base_off)
            memoT = AR.alloc([128, 4, S], BF16); memoT_r = Res("memoT")
            off2 = AR.off

            def wqm_fn(slot_ap):
                v = slot_ap[:, 0:2048].rearrange("p (k f) -> p k f", k=8)
                return [(v[:, :, :], win[:, :, 3072:3328])]
            mem_attention(l, hT, hT_r, wqm_fn, memoT, memoT_r)
            phase_barrier()
            AR.reset(off2)
            post = PostNorm()
            out_proj(l, tokT, tokT_r, memoT, memoT_r, post)

        def mla_layer(l):
            r = l // 2
            phase_barrier()
            AR.reset()
            hT = AR.alloc([128, 8, S], BF16); hT_r = Res("hT")
            tokT = AR.alloc([128, 6, S], BF16); tokT_r = [Res(f"tokT{h}") for h in range(6)]
            base_off = AR.off
            pre = PreNorm()
            for b in range(NBLK):
                pre.run(b, l * 6 + 2, hT[:, :, b * 512:(b + 1) * 512], hT_r)
            phase_barrier()
            AR.reset(base_off)
            win = mlawin_d[r].rearrange("(k p) f -> p k f", p=128)
            cqn = AR.alloc([128, 2, S], BF16); cqn_r = Res("cqn")
            ckvn = AR.alloc([128, S], BF16); ckvn_r = Res("ckvn")
            krT = AR.alloc([128, S], BF16); krT_r = Res("krT")
            qT = AR.alloc([128, S], BF16); qT_r = Res("mqT")
            kT = AR.alloc([128, S], BF16); kT_r = Res("mkT_h")
            vh = AR.alloc([128, 16, 128], BF16); vh_r = Res("mvh")
            tw = [AR.alloc([128, 2, 512], F32) for _ in range(2)]; tw_r = [Res(f"mtw{i}") for i in range(2)]
            t1 = AR.alloc([128, 512], F32); t1_r = Res("mt1")
            t2 = AR.alloc([128, 512], F32); t2_r = Res("mt2")
            sq3 = AR.alloc([128, 3, 512], BF16); sq3_r = Res("sq3")
            rtmp = AR.alloc([128, 512], F32); rtmp_r = Res("mrtmp")
            rs = AR.alloc([128, 512], F32); rs_r = Res("mrs")
            NPT = 3
            pT = [AR.alloc([128, 512], BF16) for _ in range(NPT)]; pT_r = [Res(f"apT{i}") for i in range(NPT)]
            rden, rden_r = rtmp, rtmp_r

            def pA(slot_ap):
                v = slot_ap[:, 0:3584].rearrange("p (k f) -> p k f", k=8)
                return [(v[:, :, 0:416], win[:, :, 0:416]),
                        (v[:, :, 416:432], win[:, :, 400:416]), (v[:, :, 432:448], win[:, :, 384:400])]
            wA, wA_r = ring_load(pA)
            wA = wA[:, 0:3584].rearrange("p (k f) -> p k f", k=8)
            ntw = 0
            for b in range(NBLK):
                tsl = slice(b * 512, (b + 1) * 512)
                k2 = ntw % 2
                ntw += 1
                load_tab_window(tw, tw_r, 1, b, k2)
                bq = [next_bank() for _ in range(3)]
                for ci in range(3):
                    bk, bkr = bq[ci]
                    for k in range(8):
                        op("pe", "matmul", [wA_r, hT_r], [bkr], out=bk[:, :], lhsT=wA[:, k, ci * 128:(ci + 1) * 128],
                           rhs=hT[:, k, tsl], start=(k == 0), stop=(k == 7))
                    op("act", "activation", [bkr], [sq3_r], out=sq3[:, ci, :], in_=bk[:, :], func=AF.Square)
                rstd_from_sq(sq3[:, 0:2, :], [sq3_r], 2, 1.0 / 256, rtmp, rtmp_r, rs, rs_r)
                for ci in range(2):
                    bk, bkr = bq[ci]
                    op("dve", "scalar_tensor_tensor", [bkr, vcol_r, rs_r], [cqn_r], out=cqn[:, ci, tsl], in0=bk[:, :],
                       scalar=vcol[:, 12 + r * 2 + ci:13 + r * 2 + ci], in1=rs[:, :], op0=ALU.mult, op1=ALU.mult)
                rstd_from_sq(sq3[:, 2:3, :], [sq3_r], 1, 1.0 / 128, rtmp, rtmp_r, rs, rs_r)
                bk, bkr = bq[2]
                op("dve", "scalar_tensor_tensor", [bkr, vcol_r, rs_r], [ckvn_r], out=ckvn[:, tsl], in0=bk[:, :],
                   scalar=vcol[:, 16 + r:17 + r], in1=rs[:, :], op0=ALU.mult, op1=ALU.mult)
                b0, b0r = next_bank()
                b1, b1r = next_bank()
                for (bk, bkr, o2) in ((b0, b0r, 384), (b1, b1r, 416)):
                    for k in range(8):
                        op("pe", "matmul", [wA_r, hT_r], [bkr], out=bk[0:32, :], lhsT=wA[:, k, o2:o2 + 32],
                           rhs=hT[:, k, tsl], start=(k == 0), stop=(k == 7))
                op("dve", "tensor_tensor", [b0r, tw_r[k2]], [t1_r], out=t1[0:32, :], in0=b0[0:32, :],
                   in1=tw[k2][0:32, 0, :], op=ALU.mult)
                op("dve", "tensor_tensor", [b1r, tw_r[k2]], [t2_r], out=t2[0:32, :], in0=b1[0:32, :],
                   in1=tw[k2][0:32, 1, :], op=ALU.mult)
                op("dve", "tensor_tensor", [t1_r, t2_r], [krT_r], out=krT[0:32, tsl], in0=t1[0:32, :], in1=t2[0:32, :],
                   op=ALU.add)
            wuq = mlawuq_d[r].rearrange("(k p) (h f) -> p k h f", p=128, f=96)
            def pQ(slot_ap):
                v = slot_ap[:, 0:2304].rearrange("p (k two h f) -> p k two h f", k=2, two=2, h=6)
                prs = []
                for k in range(2):
                    prs += [(v[:, k, 0, :, :], wuq[:, k, :, :]),
                            (v[:, k, 1, :, 0:64], wuq[:, k, :, 0:64]),
                            (v[:, k, 1, :, 64:80], wuq[:, k, :, 80:96]), (v[:, k, 1, :, 80:96], wuq[:, k, :, 64:80])]
                return prs
            wQ, wQ_r = ring_load(pQ)
            wQ = wQ[:, 0:2304].rearrange("p (k two h f) -> p k two h f", k=2, two=2, h=6)
            def pKV(slot_ap):
                return [(slot_ap[:, 0:1152], mlawukv_d[r])]
            wKV, wKV_r = ring_load(pKV)
            SCALE = 96.0 ** -0.5
            npt = 0
            nit = 0
            for h in range(6):
                for b in range(NBLK):
                    tsl = slice(b * 512, (b + 1) * 512)
                    k2 = ntw % 2
                    ntw += 1
                    load_tab_window(tw, tw_r, 1, b, k2)
                    b0, b0r = next_bank()
                    b1, b1r = next_bank()
                    for (bk, bkr, two) in ((b0, b0r, 0), (b1, b1r, 1)):
                        for k in range(2):
                            op("pe", "matmul", [wQ_r, cqn_r], [bkr], out=bk[0:96, :], lhsT=wQ[:, k, two, h, :],
                               rhs=cqn[:, k, tsl], start=(k == 0), stop=(k == 1))
                    op("act", "copy", [b0r], [qT_r], out=qT[0:64, tsl], in_=b0[0:64, :])
                    op("dve", "tensor_tensor", [b0r, tw_r[k2]], [t1_r], out=t1[64:96, :], in0=b0[64:96, :],
                       in1=tw[k2][64:96, 0, :], op=ALU.mult)
                    op("dve", "tensor_tensor", [b1r, tw_r[k2]], [t2_r], out=t2[64:96, :], in0=b1[64:96, :],
                       in1=tw[k2][64:96, 1, :], op=ALU.mult)
                    op("dve", "tensor_tensor", [t1_r, t2_r], [qT_r], out=qT[64:96, tsl], in0=t1[64:96, :],
                       in1=t2[64:96, :], op=ALU.add)
                    bk, bkr = next_bank()
                    op("pe", "matmul", [wKV_r, ckvn_r], [bkr], out=bk[0:64, :], lhsT=wKV[:, h * 192:h * 192 + 64],
                       rhs=ckvn[:, tsl], start=True, stop=True)
                    op("act", "copy", [bkr], [kT_r], out=kT[0:64, tsl], in_=bk[0:64, :])
                sc.dma("sp", kT_s, [(kT[64:96, :], krT[0:32, :])], reads=[krT_r], writes=[kT_r])
                for tq in range(4):
                    bk, bkr = next_bank()
                    for j in range(4):
                        t = tq * 4 + j
                        op("pe", "matmul", [wKV_r, ckvn_r], [bkr], out=bk[:, j * 128:(j + 1) * 128],
                           lhsT=ckvn[:, t * 128:(t + 1) * 128], rhs=wKV[:, h * 192 + 64:h * 192 + 192],
                           start=True, stop=True)
                    op("act", "copy", [bkr], [vh_r], out=vh[:, tq * 4:tq * 4 + 4, :],
                       in_=bk[:, :].rearrange("p (j q) -> p j q", j=4))
                for b in range(NBLK):
                    tsl = slice(b * 512, (b + 1) * 512)
                    ni = 4 + 2 * (nit % 2)
                    nit += 1
                    bn, bnr = banks[ni], bank_r[ni]
                    bd, bdr = banks[ni + 1], bank_r[ni + 1]
                    def s_mm(kc, npt0=npt):
                        si = (npt0 + kc) % 4
                        bs, bsr = banks[si], bank_r[si]
                        op("pe", "matmul", [kT_r, qT_r], [bsr], out=bs[:, :], lhsT=kT[0:96, kc * 128:(kc + 1) * 128],
                           rhs=qT[0:96, tsl], start=True, stop=True)
                    s_mm(0)
                    for kc in range(16):
                        si = (npt + kc) % 4
                        pi = (npt + kc) % NPT
                        bs, bsr = banks[si], bank_r[si]
                        if kc + 1 < 16:
                            s_mm(kc + 1)
                        op("act", "activation", [bsr], [pT_r[pi]], out=pT[pi][:, :], in_=bs[:, :], func=AF.Exp,
                           scale=SCALE)
                        op("pe", "matmul", [vh_r, pT_r[pi]], [bnr], out=bn[:, :], lhsT=vh[:, kc, :], rhs=pT[pi][:, :],
                           start=(kc == 0), stop=(kc == 15))
                        op("pe", "matmul", [ones_r, pT_r[pi]], [bdr], out=bd[:, :], lhsT=ones_bf[:, :], rhs=pT[pi][:, :],
                           start=(kc == 0), stop=(kc == 15))
                    npt += 16
                    op("dve", "reciprocal", [bdr], [rden_r], out=rden[:, :], in_=bd[:, :])
                    op("dve", "tensor_tensor", [bnr, rden_r], [tokT_r[h]], out=tokT[:, h, tsl], in0=bn[:, :],
                       in1=rden[:, :], op=ALU.mult)
            bank_n[0] = 0
            phase_barrier()
            AR.reset(base_off)
            memoT = AR.alloc([128, 4, S], BF16); memoT_r = Res("memoT")
            off2 = AR.off

            def wqm_fn(slot_ap):
                v = slot_ap[:, 0:2048].rearrange("p (k f) -> p k f", k=8)
                return [(v[:, :, :], win[:, :, 416:672])]
            mem_attention(l, hT, hT_r, wqm_fn, memoT, memoT_r)
            phase_barrier()
            AR.reset(off2)
            post = PostNorm()
            out_proj(l, tokT, tokT_r, memoT, memoT_r, post)

        for l in range(n_layers):
            if dbg.get("ffn1", True):
                ffn(l, 0)
            if dbg.get("mixer", True):
                if l % 2 == 0:
                    retention_layer(l)
                else:
                    mla_layer(l)
            if dbg.get("ffn2", True):
                ffn(l, 1)

        phase_barrier()
        AR.reset()
        xstage = [AR.alloc([128, D], F32) for _ in range(2)]
        xstage_r = [Res(f"xsto{i}") for i in range(2)]
        for t in range(16):
            si = t % 2
            for half in range(2):
                bk, bkr = next_bank()
                for j in range(4):
                    c = half * 4 + j
                    op("pe", "transpose", [xT_r[(c, t // 4)], ident_r], [bkr], out=bk[:, j * 128:(j + 1) * 128],
                       in_=xT[:, c, t * 128:(t + 1) * 128], identity=ident[:, :])
                if half == 0:
                    op("act", "copy", [bkr], [xstage_r[si]], out=xstage[si][:, 0:512], in_=bk[:, :])
                else:
                    op("dve", "tensor_copy", [bkr], [xstage_r[si]], out=xstage[si][:, 512:1024], in_=bk[:, :])
            sc.dma("sp", xsto_s[si], [(out_d[t * 128:(t + 1) * 128, :], xstage[si][:, :])], reads=[xstage_r[si]])
        for ds in xsto_s:
            sc._wait("sp", ds, ds.total)

        for e in Sched.ENG:
            sc.replay(e, None)
        with nc.Block() as block:
            @block.tensor
            def _(eng):
                sc.emit("pe", eng)

            @block.scalar
            def _(eng):
                sc.emit("act", eng)

            @block.vector
            def _(eng):
                sc.emit("dve", eng)

            @block.gpsimd
            def _(eng):
                sc.emit("pool", eng)

            @block.sync
            def _(eng):
                sc.emit("sp", eng)
    nc._used_inputs = list(used.keys())
    nc._arena_peak = AR.peak
    return nc


def _consts():
    p = np.arange(128)
    cols = np.zeros((128, 8), np.float32)
    cols[:, 0] = (1.0 / (10000.0 ** (np.arange(0, 128, 2, dtype=np.float32) / 128.0)))[p % 64]
    cols[:, 1] = (1.0 / (10000.0 ** (np.arange(0, 32, 2, dtype=np.float32) / 32.0)))[p % 16]
    cols[:, 2] = np.where(p < 64, -1.0, 1.0)
    cols[:, 3] = np.where((p % 32) < 16, -1.0, 1.0)
    cols[:, 4] = 127.0 - p
    cols[:, 5] = p
    tabs = np.zeros((128, 4, 128), np.float32)
    tl = np.arange(128, dtype=np.float32)
    tabs[:, 0, :] = tl[None, :] + 1.0
    tabs[:, 1, :] = 128.0 - tl[None, :]
    d = tl[None, :] - p[:, None].astype(np.float32)
    tabs[:, 2, :] = np.maximum(d, 0.0)
    tabs[:, 3, :] = np.maximum(-d, 0.0)
    return {"c_ident": np.eye(128, dtype=np.float32), "c_cols": cols, "c_tabs": tabs}


def kernel(x, mem, positions, norm_gains, ffn_w_gu, ffn_w_down, w_o, mem_norm, mem_w_kv,
           ret_w_in, ret_log_decay, ret_head_norm, mla_w_in, mla_q_norm, mla_kv_norm,
           mla_w_uq, mla_w_ukv, _dbg=None):
    f = lambda a: np.ascontiguousarray(np.asarray(a, dtype=np.float32))
    shared = {
        "norm_gains": f(norm_gains).reshape(DEPTH * 6, D),
        "ffn_w_gu": f(ffn_w_gu), "ffn_w_down": f(ffn_w_down), "w_o": f(w_o),
        "mem_norm": f(mem_norm).reshape(8, 128), "mem_w_kv": f(mem_w_kv),
        "ret_w_in": f(ret_w_in), "ret_log_decay": f(ret_log_decay).reshape(1, 24),
        "ret_head_norm": f(ret_head_norm).reshape(12, 128),
        "mla_w_in": f(mla_w_in), "mla_q_norm": f(mla_q_norm).reshape(4, 128),
        "mla_kv_norm": f(mla_kv_norm).reshape(2, 128),
        "mla_w_uq": f(mla_w_uq), "mla_w_ukv": f(mla_w_ukv),
    }
    shared.update(_consts())
    x = f(x)
    mem = f(mem)
    pos = np.ascontiguousarray(np.asarray(positions, dtype=np.int32))
    nc = build_program(_dbg)
    if _dbg and "nl_w" in _dbg:
        for k in ("ffn_w_gu", "ffn_w_down", "w_o", "mem_w_kv"):
            shared[k] = np.ascontiguousarray(shared[k][:_dbg["nl_w"]])
    in_maps = []
    for b in range(8):
        m = {k: v for k, v in shared.items() if k in nc._used_inputs}
        m["x"] = x[b]
        if "mem" in nc._used_inputs:
            m["mem"] = mem[b]
        if "positions" in nc._used_inputs:
            m["positions"] = pos[b].reshape(1, S)
        in_maps.append(m)
    res = run_bass_kernel_spmd(nc, in_maps, core_ids=list(range(8)))
    return np.stack([np.asarray(r["out"], dtype=np.float32) for r in res.results], axis=0)
```

```python
from contextlib import ExitStack
import numpy as np
import concourse.bass as bass
import concourse.mybir as mybir
from concourse.bass_utils import run_bass_kernel_spmd

F32 = mybir.dt.float32
BF16 = mybir.dt.bfloat16
I32 = mybir.dt.int32
AF = mybir.ActivationFunctionType
ALU = mybir.AluOpType

D = 1024
S = 2048
DFF = 2816
NF = DFF // 128
DEPTH = 4
EPS = 1e-6
TG = 512
NBLK = S // 512


class Res:
    __slots__ = ("name", "w", "r")

    def __init__(self, name):
        self.name = name
        self.w = None
        self.r = []


class DmaSem:
    def __init__(self, sem):
        self.sem = sem
        self.total = 0


class Sched:
    ENG = ("pe", "act", "dve", "pool", "sp")

    def __init__(self, sems):
        self.sem = sems
        self.prog = {e: [] for e in self.ENG}
        self.nops = {e: 0 for e in self.ENG}
        self.waited = {e: {} for e in self.ENG}
        self.needed = set()

    def _wait(self, e, key, val):
        if self.waited[e].get(key, 0) >= val:
            return
        self.waited[e][key] = val
        if isinstance(key, DmaSem):
            self.prog[e].append(("wd", key, val))
        else:
            self.needed.add((key, val))
            self.prog[e].append(("w", key, val))

    def _deps(self, e, reads, writes):
        for r in reads:
            if r.w is not None:
                k, v = r.w
                if not (k == e and e == "pe"):
                    self._wait(e, k, v)
        for w in writes:
            if w.w is not None:
                k, v = w.w
                if not (k == e and e == "pe"):
                    self._wait(e, k, v)
            for (k, v) in w.r:
                if not (k == e and e == "pe"):
                    self._wait(e, k, v)

    def op(self, e, meth, reads=(), writes=(), **kw):
        self._deps(e, reads, writes)
        self.nops[e] += 1
        c = self.nops[e]
        self.prog[e].append(("op", (meth, kw), c))
        for w in writes:
            w.w = (e, c)
            w.r = []
        for r in reads:
            r.r.append((e, c))
        return c

    def dma(self, e, dsem, pairs, reads=(), writes=()):
        self._deps(e, reads, writes)
        for (o, i) in pairs:
            dsem.total += 16
            self.prog[e].append(("dma", (o, i), dsem))
        ev = (dsem, dsem.total)
        for w in writes:
            w.w = ev
            w.r = []
        for r in reads:
            r.r.append(ev)

    def wait_all(self, e, items):
        for r in items:
            if r.w is not None:
                self._wait(e, *r.w)
            for (k, v) in r.r:
                self._wait(e, k, v)

    def barrier(self, engs=("pe", "act", "dve", "pool")):
        for e in engs:
            for f in engs:
                if f != e and self.nops[f] > 0:
                    self._wait(e, f, self.nops[f])

    def replay(self, e, eng):
        cnt = 0
        num = {}
        for it in self.prog[e]:
            if it[0] == "op" and (e, it[2]) in self.needed:
                cnt += 1
                num[it[2]] = cnt
        self._num = getattr(self, "_num", {})
        self._num[e] = num

    def emit(self, e, eng):
        for it in self.prog[e]:
            k = it[0]
            if k == "w":
                _, key, val = it
                eng.wait_ge(self.sem[key], self._num[key][val])
            elif k == "wd":
                _, ds, val = it
                eng.wait_ge(ds.sem, val)
            elif k == "op":
                _, (meth, kw), c = it
                ins = getattr(eng, meth)(**kw)
                if (e, c) in self.needed:
                    ins.then_inc(self.sem[e], 1)
            elif k == "dma":
                _, (o, i), ds = it
                eng.dma_start(out=o, in_=i).then_inc(ds.sem, 16)


import math


class Arena:
    def __init__(self, t, nwords):
        self.t = t
        self.n = nwords
        self.off = 0
        self.peak = 0

    def reset(self, off=0):
        self.off = off

    def alloc(self, shape, dt):
        free = 1
        for d in shape[1:]:
            free *= d
        esz = 4 if dt in (F32, I32) else 2
        words = (free * esz + 3) // 4
        words = (words + 7) // 8 * 8
        assert self.off + words <= self.n, f"arena overflow: {self.off}+{words} > {self.n}"
        v = self.t[:, self.off:self.off + words]
        self.off += words
        self.peak = max(self.peak, self.off)
        if dt != F32:
            v = v.bitcast(dt)
        v = v[:, 0:free]
        if len(shape) == 3:
            v = v.rearrange("p (a b) -> p a b", a=shape[1])
        elif len(shape) == 4:
            v = v.rearrange("p (a b c) -> p a b c", a=shape[1], b=shape[2])
        return v


def build_program(dbg=None):
    dbg = dbg or {}
    n_layers = dbg.get("n_layers", DEPTH)
    nc = bass.Bass("TRN2", target_bir_lowering=False)

    used = {}
    NL = dbg.get("nl_w", DEPTH)
    SHAPES = {
        "x": ([S, D], F32), "mem": ([256, D], F32), "positions": ([1, S], I32),
        "norm_gains": ([DEPTH * 6, D], F32), "ffn_w_gu": ([NL, 2, D, 2 * DFF], F32),
        "ffn_w_down": ([NL, 2, DFF, D], F32), "w_o": ([NL, D, D], F32),
        "mem_norm": ([8, 128], F32), "mem_w_kv": ([NL, D, 512], F32),
        "ret_w_in": ([2, D, 3328], F32), "ret_log_decay": ([1, 24], F32),
        "ret_head_norm": ([12, 128], F32), "mla_w_in": ([2, D, 672], F32),
        "mla_q_norm": ([4, 128], F32), "mla_kv_norm": ([2, 128], F32),
        "mla_w_uq": ([2, 256, 576], F32), "mla_w_ukv": ([2, 128, 1152], F32),
        "c_ident": ([128, 128], F32), "c_cols": ([128, 8], F32), "c_tabs": ([128, 4, 128], F32),
    }

    class _Lazy:
        def __init__(self, name):
            self.name = name

        def get(self):
            if self.name not in used:
                shp, dt = SHAPES[self.name]
                used[self.name] = nc.dram_tensor(self.name, list(shp), dt, kind="ExternalInput").ap()
            return used[self.name]

        def __getitem__(self, k):
            return self.get()[k]

    x_d = _Lazy("x"); mem_d = _Lazy("mem"); pos_d = _Lazy("positions"); ng_d = _Lazy("norm_gains")
    wgu_d = _Lazy("ffn_w_gu"); wdn_d = _Lazy("ffn_w_down"); wo_d = _Lazy("w_o")
    memnorm_d = _Lazy("mem_norm"); memwkv_d = _Lazy("mem_w_kv"); retwin_d = _Lazy("ret_w_in")
    retld_d = _Lazy("ret_log_decay"); rethn_d = _Lazy("ret_head_norm"); mlawin_d = _Lazy("mla_w_in")
    mlaqn_d = _Lazy("mla_q_norm"); mlakvn_d = _Lazy("mla_kv_norm"); mlawuq_d = _Lazy("mla_w_uq")
    mlawukv_d = _Lazy("mla_w_ukv"); ident_d = _Lazy("c_ident"); ccols_d = _Lazy("c_cols"); ctabs_d = _Lazy("c_tabs")
    out_d = nc.dram_tensor("out", [S, D], F32, kind="ExternalOutput").ap()
    tabs_d = nc.dram_tensor("tabs_scratch", [4, 128, S], F32).ap()

    es = ExitStack()
    with es:
        def sb(name, shape, dt):
            return es.enter_context(nc.sbuf_tensor(name, list(shape), dt))

        def sem(name):
            return es.enter_context(nc.semaphore(name))

        sems = {e: sem("s_" + e) for e in Sched.ENG}
        sc = Sched(sems)
        op = sc.op

        xT = sb("xT", [128, 8, S], F32)
        xT_r = {(c, b): Res(f"xT{c}_{b}") for c in range(8) for b in range(NBLK)}
        NSLOT = 3
        SLOTE = 4096
        ring = sb("ring", [128, NSLOT, SLOTE], BF16)
        ring_r = [Res(f"ring{i}") for i in range(NSLOT)]
        ring_s = [DmaSem(sem(f"s_ring{i}")) for i in range(NSLOT)]
        ring_n = [0]
        ident = sb("ident", [128, 128], F32); ident_r = Res("ident")
        ident_bf = sb("ident_bf", [128, 128], BF16)
        ones_bf = sb("ones_bf", [128, 128], BF16); ones_r = Res("ones")
        gcol = sb("gcol", [128, 8, 24], F32)
        gcolh = sb("gcolh", [128, 8, 24], F32)
        gcol_r = Res("gcol")
        vcol = sb("vcol", [128, 26], F32); vcol_r = Res("vcol")
        ccols = sb("ccols", [128, 8], F32); ccols_r = Res("ccols")
        ctabs = sb("ctabs", [128, 4, 128], F32); ctabs_r = Res("ctabs")
        ldc = sb("ldc", [128, 24], F32); ldc_r = Res("ldc")
        eps_col = sb("eps_col", [128, 1], F32); eps_r = Res("eps")
        memT = sb("memT", [128, 8, 256], BF16); memT_r = Res("memT")
        all_dsems = []

        def one_sem():
            d = DmaSem(sem(f"s_one{len(all_dsems)}"))
            all_dsems.append(d)
            return d
        misc2_s = DmaSem(sem("s_misc2"))
        xst_s = [one_sem(), one_sem()]
        xsto_s = [one_sem(), one_sem()]
        kT_s = one_sem()
        tab_s = [DmaSem(sem(f"s_tab{i}")) for i in range(2)]
        out_s = DmaSem(sem("s_out"))
        ARENA_W = dbg.get("arena_words", 27400)
        arena_t = sb("arena", [128, ARENA_W], F32)
        AR = Arena(arena_t, ARENA_W)

        banks = [es.enter_context(nc.psum_tensor(f"bank{i}", [128, 512], F32)) for i in range(8)]
        bank_r = [Res(f"bank{i}") for i in range(8)]
        bank_n = [0]

        def next_bank():
            i = bank_n[0] % 8
            bank_n[0] += 1
            return banks[i], bank_r[i]

        def ring_load(pairs_fn):
            i = ring_n[0] % NSLOT
            ring_n[0] += 1
            sc.dma("pool", ring_s[i], pairs_fn(ring[:, i]), writes=[ring_r[i]])
            return ring[:, i], ring_r[i]

        def phase_barrier():
            for ds in [misc2_s, tab_s[0], tab_s[1]] + all_dsems:
                if ds.total > 0:
                    sc._wait("sp", ds, ds.total)
            op("sp", "nop", [], [])
            sc.barrier(("pe", "act", "dve", "sp"))

        sc.dma("sp", one_sem(), [(ident[:, :], ident_d[:, :]), (ccols[:, :], ccols_d[:, :]),
                              (ctabs[:, :, :], ctabs_d[:, :, :])], writes=[ident_r, ccols_r, ctabs_r])
        sc.dma("sp", one_sem(), [(ldc[:, :], retld_d[0:1, :].partition_broadcast(128))], writes=[ldc_r])
        op("dve", "memset", [], [ones_r], ap=ones_bf[:, :], constant=1.0)
        op("dve", "memset", [], [eps_r], ap=eps_col[:, :], constant=EPS)
        op("dve", "tensor_copy", [ident_r], [ones_r], out=ident_bf[:, :], in_=ident[:, :])

        AR.reset()
        gstage = AR.alloc([128, D], F32); gstage_r = Res("gstage")
        vstage = AR.alloc([128, 128], F32); vstage_r = Res("vstage")
        sc.dma("sp", one_sem(), [(gstage[0:24, :], ng_d[:, :])], writes=[gstage_r])
        sc.dma("sp", one_sem(), [(vstage[0:12, :], rethn_d[:, :]), (vstage[12:16, :], mlaqn_d[:, :]),
                              (vstage[16:18, :], mlakvn_d[:, :]), (vstage[18:26, :], memnorm_d[:, :])],
               writes=[vstage_r])
        for c in range(8):
            bk, bkr = next_bank()
            op("pe", "transpose", [gstage_r, ident_r], [bkr], out=bk[:, 0:24],
               in_=gstage[0:24, c * 128:(c + 1) * 128], identity=ident[0:24, 0:24])
            op("dve", "tensor_copy", [bkr], [gcol_r], out=gcol[:, c, :], in_=bk[:, 0:24])
            op("act", "mul", [bkr], [gcol_r], out=gcolh[:, c, :], in_=bk[:, 0:24], mul=0.5)
        bk, bkr = next_bank()
        op("pe", "transpose", [vstage_r, ident_r], [bkr], out=bk[:, 0:26], in_=vstage[0:26, :],
           identity=ident[0:26, 0:26])
        op("dve", "tensor_copy", [bkr], [vcol_r], out=vcol[:, :], in_=bk[:, 0:26])

        PI = math.pi
        TWO_PI = 2.0 * math.pi
        C1 = 6.28125
        C2 = TWO_PI - C1
        PI_LO = 3.1415925
        if dbg.get("tables", True):
            posi = AR.alloc([128, S], I32); posi_r = Res("posi")
            posf = AR.alloc([128, S], F32); posf_r = Res("posf")
            ang = AR.alloc([128, S], F32); ang_r = Res("ang")
            tv = AR.alloc([128, S], F32); tv_r = Res("tv")
            ki = AR.alloc([128, S], I32); ki_r = Res("ki")
            kf = AR.alloc([128, S], F32); kf_r = Res("kf")
            sc.dma("sp", one_sem(), [(posi[:, :], pos_d[0:1, :].partition_broadcast(128))], writes=[posi_r])
            op("dve", "tensor_copy", [posi_r], [posf_r], out=posf[:, :], in_=posi[:, :])
            ti = 0
            for kind in range(2):
                for cs in range(2):
                    if cs == 0:
                        op("dve", "tensor_scalar", [posf_r, ccols_r], [ang_r], out=ang[:, :], in0=posf[:, :],
                           scalar1=ccols[:, kind:kind + 1], scalar2=PI / 2, op0=ALU.mult, op1=ALU.add)
                    else:
                        op("dve", "tensor_scalar", [posf_r, ccols_r], [ang_r], out=ang[:, :], in0=posf[:, :],
                           scalar1=ccols[:, kind:kind + 1], scalar2=None, op0=ALU.mult)
                    op("dve", "tensor_scalar", [ang_r], [tv_r], out=tv[:, :], in0=ang[:, :],
                       scalar1=1.0 / TWO_PI, scalar2=0.5, op0=ALU.mult, op1=ALU.add)
                    op("dve", "tensor_copy", [tv_r], [ki_r], out=ki[:, :], in_=tv[:, :])
                    op("dve", "tensor_copy", [ki_r], [kf_r], out=kf[:, :], in_=ki[:, :])
                    op("dve", "scalar_tensor_tensor", [kf_r, ang_r], [tv_r], out=tv[:, :], in0=kf[:, :],
                       scalar=-C1, in1=ang[:, :], op0=ALU.mult, op1=ALU.add)
                    op("dve", "scalar_tensor_tensor", [kf_r, tv_r], [ang_r], out=ang[:, :], in0=kf[:, :],
                       scalar=-C2, in1=tv[:, :], op0=ALU.mult, op1=ALU.add)
                    op("dve", "tensor_scalar", [ang_r], [tv_r], out=tv[:, :], in0=ang[:, :],
                       scalar1=-PI, scalar2=TWO_PI, op0=ALU.is_lt, op1=ALU.mult)
                    op("dve", "tensor_tensor", [ang_r, tv_r], [ang_r], out=ang[:, :], in0=ang[:, :], in1=tv[:, :],
                       op=ALU.add)
                    op("dve", "tensor_scalar", [ang_r], [tv_r], out=tv[:, :], in0=ang[:, :],
                       scalar1=PI, scalar2=-TWO_PI, op0=ALU.is_gt, op1=ALU.mult)
                    op("dve", "tensor_tensor", [ang_r, tv_r], [ang_r], out=ang[:, :], in0=ang[:, :], in1=tv[:, :],
                       op=ALU.add)
                    op("dve", "tensor_scalar", [ang_r], [ang_r], out=ang[:, :], in0=ang[:, :],
                       scalar1=PI_LO, scalar2=-PI_LO, op0=ALU.min, op1=ALU.max)
                    if cs == 0:
                        op("act", "activation", [ang_r], [kf_r], out=kf[:, :], in_=ang[:, :], func=AF.Sin)
                    else:
                        op("act", "activation", [ang_r, ccols_r], [kf_r], out=kf[:, :], in_=ang[:, :], func=AF.Sin,
                           scale=ccols[:, 2 + kind:3 + kind])
                    sc.dma("sp", misc2_s, [(tabs_d[ti], kf[:, :])], reads=[kf_r])
                    ti += 1
            sc._wait("sp", misc2_s, misc2_s.total)

        if dbg.get("memprep", True):
            phase_barrier()
            AR.reset()
            for mt in range(2):
                mst = AR.alloc([128, D], F32); mst_r = Res("mst")
                msq = AR.alloc([128, D], F32); msq_r = Res("msq")
                mss = AR.alloc([128, 2], F32); mss_r = Res("mss")
                sc.dma("sp", one_sem(), [(mst[:, :], mem_d[mt * 128:(mt + 1) * 128, :])], writes=[mst_r])
                op("act", "activation", [mst_r], [msq_r, mss_r], out=msq[:, :], in_=mst[:, :], func=AF.Square,
                   accum_out=mss[:, 0:1])
                op("act", "activation", [mss_r, eps_r], [mss_r], out=mss[:, 1:2], in_=mss[:, 0:1], func=AF.Sqrt,
                   bias=eps_col[:, 0:1], scale=1.0 / D)
                op("dve", "reciprocal", [mss_r], [mss_r], out=mss[:, 0:1], in_=mss[:, 1:2])
                op("dve", "tensor_scalar", [mst_r, mss_r], [msq_r], out=msq[:, :], in0=mst[:, :],
                   scalar1=mss[:, 0:1], scalar2=None, op0=ALU.mult)
                for half in range(2):
                    bk, bkr = next_bank()
                    for j in range(4):
                        c = half * 4 + j
                        op("pe", "transpose", [msq_r, ident_r], [bkr], out=bk[:, j * 128:(j + 1) * 128],
                           in_=msq[:, c * 128:(c + 1) * 128], identity=ident[:, :])
                    for j in range(4):
                        c = half * 4 + j
                        op("dve", "tensor_scalar", [bkr, vcol_r], [memT_r], out=memT[:, c, mt * 128:(mt + 1) * 128],
                           in0=bk[:, j * 128:(j + 1) * 128], scalar1=vcol[:, 18 + c:19 + c], scalar2=None,
                           op0=ALU.mult)

        phase_barrier()
        AR.reset()
        xstage = [AR.alloc([128, D], F32) for _ in range(2)]
        xstage_r = [Res(f"xst{i}") for i in range(2)]
        for t in range(16):
            si = t % 2
            sc.dma("sp", xst_s[si], [(xstage[si][:, :], x_d[t * 128:(t + 1) * 128, :])], writes=[xstage_r[si]])
            for half in range(2):
                bk, bkr = next_bank()
                for j in range(4):
                    c = half * 4 + j
                    op("pe", "transpose", [xstage_r[si], ident_r], [bkr], out=bk[:, j * 128:(j + 1) * 128],
                       in_=xstage[si][:, c * 128:(c + 1) * 128], identity=ident[:, :])
                wr = [xT_r[(half * 4 + j, t // 4)] for j in range(4)]
                o_ap = xT[:, half * 4:half * 4 + 4, t * 128:(t + 1) * 128]
                i_ap = bk[:, :].rearrange("p (j q) -> p j q", j=4)
                if half == 0:
                    op("act", "copy", [bkr], wr, out=o_ap, in_=i_ap)
                else:
                    op("dve", "tensor_copy", [bkr], wr, out=o_ap, in_=i_ap)

        def rstd_from_sq(sq_ap, sq_res, nchunks, inv_n, rtmp, rtmp_r, rs, rs_r, np_=128, lnexp=True):
            bk, bkr = next_bank()
            for c in range(nchunks):
                op("pe", "matmul", [ones_r] + sq_res, [bkr], out=bk[:, :], lhsT=ones_bf[:, :], rhs=sq_ap[:, c, :],
                   start=(c == 0), stop=(c == nchunks - 1))
            if lnexp:
                op("act", "activation", [bkr, eps_r], [rtmp_r], out=rtmp[:, :], in_=bk[:, :], func=AF.Ln,
                   bias=eps_col[:, 0:1], scale=inv_n)
                op("act", "activation", [rtmp_r], [rs_r], out=rs[:, :], in_=rtmp[:, :], func=AF.Exp, scale=-0.5)
            else:
                op("act", "activation", [bkr, eps_r], [rtmp_r], out=rtmp[:, :], in_=bk[:, :], func=AF.Sqrt,
                   bias=eps_col[:, 0:1], scale=inv_n)
                op("dve", "reciprocal", [rtmp_r], [rs_r], out=rs[:, :], in_=rtmp[:, :])

        class PostNorm:
            def __init__(self, lnexp=True):
                self.lnexp = lnexp
                self.yT = AR.alloc([128, 8, 512], F32); self.yT_r = [Res(f"yT{c}") for c in range(8)]
                self.sqb = AR.alloc([128, 8, 512], BF16); self.sqb_r = [Res(f"sqb{c}") for c in range(8)]
                self.rtmp = AR.alloc([128, 512], F32); self.rtmp_r = Res("rtmp")
                self.rs = AR.alloc([128, 512], F32); self.rs_r = Res("rs_post")
                self.upd = [AR.alloc([128, 512], F32) for _ in range(2)]
                self.upd_r = [Res(f"upd{i}") for i in range(2)]
                self.n = 0

            def take(self, c, bk, bkr):
                op("act", "copy", [bkr], [self.yT_r[c]], out=self.yT[:, c, :], in_=bk[:, :])
                if self.lnexp:
                    op("act", "activation", [bkr], [self.sqb_r[c]], out=self.sqb[:, c, :], in_=bk[:, :], func=AF.Square)
                else:
                    op("pool", "tensor_tensor", [self.yT_r[c]], [self.sqb_r[c]], out=self.sqb[:, c, :],
                       in0=self.yT[:, c, :], in1=self.yT[:, c, :], op=ALU.mult)

            def finish(self, blk, gi, half):
                tsl = slice(blk * 512, (blk + 1) * 512)
                rstd_from_sq(self.sqb, self.sqb_r, 8, 1.0 / D, self.rtmp, self.rtmp_r, self.rs, self.rs_r,
                             lnexp=self.lnexp)
                gc = gcolh if half else gcol
                for c in range(8):
                    ui = self.n % 2
                    self.n += 1
                    xr = xT_r[(c, blk)]
                    op("dve", "tensor_tensor", [self.yT_r[c], self.rs_r], [self.upd_r[ui]],
                       out=self.upd[ui][:, :], in0=self.yT[:, c, :], in1=self.rs[:, :], op=ALU.mult)
                    op("dve", "scalar_tensor_tensor", [self.upd_r[ui], gcol_r, xr], [xr], out=xT[:, c, tsl],
                       in0=self.upd[ui][:, :], scalar=gc[:, c, gi:gi + 1], in1=xT[:, c, tsl],
                       op0=ALU.mult, op1=ALU.add)

        class PreNorm:
            def __init__(self, lnexp=True):
                self.lnexp = lnexp
                self.sqa = AR.alloc([128, 8, 512], BF16); self.sqa_r = Res("sqa")
                self.rtmp = AR.alloc([128, 512], F32); self.rtmp_r = Res("rtmp_pre")
                self.rs = AR.alloc([128, 512], F32); self.rs_r = Res("rs_pre")

            def run(self, blk, gi, out_ap, out_res):
                tsl = slice(blk * 512, (blk + 1) * 512)
                xr = [xT_r[(c, blk)] for c in range(8)]
                op("act", "activation", xr, [self.sqa_r], out=self.sqa[:, :, :], in_=xT[:, :, tsl], func=AF.Square)
                rstd_from_sq(self.sqa, [self.sqa_r], 8, 1.0 / D, self.rtmp, self.rtmp_r, self.rs, self.rs_r,
                             lnexp=self.lnexp)
                for c in range(8):
                    op("dve", "scalar_tensor_tensor", [xr[c], gcol_r, self.rs_r], [out_res],
                       out=out_ap[:, c, :], in0=xT[:, c, tsl], scalar=gcol[:, c, gi:gi + 1],
                       in1=self.rs[:, :], op0=ALU.mult, op1=ALU.mult)

        def ffn(l, j):
            phase_barrier()
            AR.reset()
            gi_pre = l * 6 + (0 if j == 0 else 4)
            gi_post = l * 6 + (1 if j == 0 else 5)
            wgu = wgu_d[l, j].rearrange("(k p) f -> p k f", p=128)
            wdn = wdn_d[l, j].rearrange("(fc p) d -> p fc d", p=128)
            pre = PreNorm()
            post = PostNorm()
            hTg = [AR.alloc([128, 8, 512], BF16) for _ in range(2)]
            hTg_r = [Res(f"hTg{i}") for i in range(2)]
            actT = AR.alloc([128, NF, 512], BF16)
            actT_r = [Res(f"actT{f}") for f in range(NF)]
            silu_t = [AR.alloc([128, 512], BF16) for _ in range(2)]
            silu_r = [Res(f"silu{i}") for i in range(2)]
            nsl = 0
            pre.run(0, gi_pre, hTg[0], hTg_r[0])
            for g in range(NBLK):
                blk = g
                hb = g % 2
                tsl = slice(blk * 512, (blk + 1) * 512)
                if dbg.get("ffn_stage", 3) == 1:
                    for c in range(8):
                        op("act", "copy", [hTg_r[hb]], [xT_r[(c, blk)]], out=xT[:, c, tsl], in_=hTg[hb][:, c, :])
                    continue
                for fp in range(NF // 2):
                    def pairs(slot_ap, fp=fp):
                        v = slot_ap.rearrange("p (k two f) -> p k two f", k=8, two=2)
                        return [(v[:, :, 0, :], wgu[:, :, fp * 256:(fp + 1) * 256]),
                                (v[:, :, 1, :], wgu[:, :, DFF + fp * 256:DFF + (fp + 1) * 256])]
                    wslot, wres = ring_load(pairs)
                    wv = wslot.rearrange("p (k two f) -> p k two f", k=8, two=2)
                    for ff in range(2):
                        f = fp * 2 + ff
                        bg, bgr = next_bank()
                        bu, bur = next_bank()
                        for (which, bk, bkr) in ((0, bg, bgr), (1, bu, bur)):
                            for k in range(8):
                                op("pe", "matmul", [wres, hTg_r[hb]], [bkr], out=bk[:, :],
                                   lhsT=wv[:, k, which, ff * 128:(ff + 1) * 128], rhs=hTg[hb][:, k, :],
                                   start=(k == 0), stop=(k == 7))
                        si = nsl % 2
                        nsl += 1
                        op("act", "activation", [bgr], [silu_r[si]], out=silu_t[si][:, :], in_=bg[:, :], func=AF.Silu)
                        op("dve", "tensor_tensor", [bur, silu_r[si]], [actT_r[f]], out=actT[:, f, :], in0=bu[:, :],
                           in1=silu_t[si][:, :], op=ALU.mult)
                if dbg.get("ffn_stage", 3) == 2:
                    for c in range(8):
                        op("act", "copy", [actT_r[c]], [xT_r[(c, blk)]], out=xT[:, c, tsl], in_=actT[:, c, :])
                    continue
                if g + 1 < NBLK:
                    pre.run(g + 1, gi_pre, hTg[(g + 1) % 2], hTg_r[(g + 1) % 2])
                for dh in range(2):
                    bks = [next_bank() for _ in range(4)]
                    f0 = 0
                    while f0 < NF:
                        nfc = min(8, NF - f0)

                        def pairs(slot_ap, f0=f0, nfc=nfc, dh=dh):
                            v = slot_ap.rearrange("p (fc d) -> p fc d", d=512)
                            return [(v[:, 0:nfc, :], wdn[:, f0:f0 + nfc, dh * 512:(dh + 1) * 512])]
                        wslot, wres = ring_load(pairs)
                        wv = wslot.rearrange("p (fc d) -> p fc d", d=512)
                        for fi in range(nfc):
                            f = f0 + fi
                            for dc in range(4):
                                bk, bkr = bks[dc]
                                op("pe", "matmul", [wres, actT_r[f]], [bkr], out=bk[:, :],
                                   lhsT=wv[:, fi, dc * 128:(dc + 1) * 128], rhs=actT[:, f, :],
                                   start=(f == 0), stop=(f == NF - 1))
                        f0 += nfc
                    for dc in range(4):
                        bk, bkr = bks[dc]
                        post.take(dh * 4 + dc, bk, bkr)
                if dbg.get("ffn_stage", 3) == 25:
                    for c in range(8):
                        op("act", "copy", [post.yT_r[c]], [xT_r[(c, blk)]], out=xT[:, c, tsl], in_=post.yT[:, c, :])
                    continue
                post.finish(blk, gi_post, True)

        def load_tab_window(tw, tw_r, which, blk, k):
            sc.dma("sp", tab_s[k], [(tw[k][:, 0, :], tabs_d[2 * which, :, blk * 512:(blk + 1) * 512]),
                                    (tw[k][:, 1, :], tabs_d[2 * which + 1, :, blk * 512:(blk + 1) * 512])],
                   writes=[tw_r[k]])

        def mem_attention(l, hT, hT_r, wqm_fn, memoT, memoT_r):
            mkT = AR.alloc([128, 4, 256], BF16); mkT_r = Res("mkT")
            mv = AR.alloc([128, 2, 256], BF16); mv_r = Res("mv")
            qmsb = [AR.alloc([128, 512], BF16) for _ in range(2)]; qmsb_r = [Res(f"qmsb{i}") for i in range(2)]
            pT = [AR.alloc([128, 2, 512], BF16) for _ in range(2)]; pT_r = [Res(f"mpT{i}") for i in range(2)]
            rden = AR.alloc([128, 512], F32); rden_r = Res("mrden")
            lden = AR.alloc([128, 512], F32); lden_r = Res("mlden")

            def pairs_kv(slot_ap):
                v = slot_ap.rearrange("p (k f) -> p k f", k=8)
                return [(v[:, :, :], memwkv_d[l].rearrange("(k p) f -> p k f", p=128))]
            wkv, wkv_r = ring_load(pairs_kv)
            wkv = wkv.rearrange("p (k f) -> p k f", k=8)
            wqm, wqm_r = ring_load(wqm_fn)
            wqm = wqm[:, 0:2048].rearrange("p (k f) -> p k f", k=8)
            for a in range(4):
                bk, bkr = next_bank()
                for k in range(8):
                    op("pe", "matmul", [wkv_r, memT_r], [bkr], out=bk[0:64, 0:256], lhsT=wkv[:, k, a * 64:(a + 1) * 64],
                       rhs=memT[:, k, :], start=(k == 0), stop=(k == 7))
                op("act", "copy", [bkr], [mkT_r], out=mkT[0:64, a, :], in_=bk[0:64, 0:256])
            for mt in range(2):
                bk, bkr = next_bank()
                for k in range(8):
                    op("pe", "matmul", [wkv_r, memT_r], [bkr], out=bk[:, 0:256], lhsT=memT[:, k, mt * 128:(mt + 1) * 128],
                       rhs=wkv[:, k, 256:512], start=(k == 0), stop=(k == 7))
                op("act", "copy", [bkr], [mv_r], out=mv[:, mt, :], in_=bk[:, 0:256])
            items = [(b, a) for b in range(NBLK) for a in range(4)]
            sbank = {}

            def st1(i):
                b, a = items[i]
                tsl = slice(b * 512, (b + 1) * 512)
                bq, bqr = next_bank()
                for k in range(8):
                    op("pe", "matmul", [wqm_r, hT_r], [bqr], out=bq[0:64, :], lhsT=wqm[:, k, a * 64:(a + 1) * 64],
                       rhs=hT[:, k, tsl], start=(k == 0), stop=(k == 7))
                op("act", "copy", [bqr], [qmsb_r[i % 2]], out=qmsb[i % 2][0:64, :], in_=bq[0:64, :])

            def st2(i):
                b, a = items[i]
                for mt in range(2):
                    bs, bsr = next_bank()
                    op("pe", "matmul", [mkT_r, qmsb_r[i % 2]], [bsr], out=bs[:, :],
                       lhsT=mkT[0:64, a, mt * 128:(mt + 1) * 128], rhs=qmsb[i % 2][0:64, :], start=True, stop=True)
                    op("act", "activation", [bsr], [pT_r[i % 2]], out=pT[i % 2][:, mt, :], in_=bs[:, :], func=AF.Exp,
                       scale=0.125)

            def st3(i):
                b, a = items[i]
                tsl = slice(b * 512, (b + 1) * 512)
                i2 = i % 2
                bn, bnr = next_bank()
                bd, bdr = next_bank()
                for mt in range(2):
                    op("pe", "matmul", [mv_r, pT_r[i2]], [bnr], out=bn[0:64, :], lhsT=mv[:, mt, a * 64:(a + 1) * 64],
                       rhs=pT[i2][:, mt, :], start=(mt == 0), stop=(mt == 1))
                for mt in range(2):
                    op("pe", "matmul", [ones_r, pT_r[i2]], [bdr], out=bd[0:64, :], lhsT=ones_bf[:, 0:64],
                       rhs=pT[i2][:, mt, :], start=(mt == 0), stop=(mt == 1))
                op("act", "activation", [bdr], [lden_r], out=lden[0:64, :], in_=bd[0:64, :], func=AF.Ln)
                op("act", "activation", [lden_r], [rden_r], out=rden[0:64, :], in_=lden[0:64, :], func=AF.Exp, scale=-1.0)
                op("dve", "tensor_tensor", [bnr, rden_r], [memoT_r], out=memoT[0:64, a, tsl], in0=bn[0:64, :],
                   in1=rden[0:64, :], op=ALU.mult)
            n = len(items)
            for step in range(n + 2):
                if step < n:
                    st1(step)
                if 0 <= step - 1 < n:
                    st2(step - 1)
                if 0 <= step - 2 < n:
                    st3(step - 2)

        def out_proj(l, tokT, tokT_r, memoT, memoT_r, post):
            wo = wo_d[l]
            def p1(slot_ap):
                v = slot_ap.rearrange("p (h d) -> p h d", h=4)
                return [(v[:, :, :], wo[0:512, :].rearrange("(h p) d -> p h d", p=128))]
            def p2(slot_ap):
                v = slot_ap.rearrange("p (h d) -> p h d", h=4)
                return [(v[:, 0:2, :], wo[512:768, :].rearrange("(h p) d -> p h d", p=128)),
                        (v[0:64, 2:4, :], wo[768:896, :].rearrange("(a p) d -> p a d", p=64))]
            def p3(slot_ap):
                v = slot_ap.rearrange("p (h d) -> p h d", h=4)
                return [(v[0:64, 0:2, :], wo[896:1024, :].rearrange("(a p) d -> p a d", p=64))]
            s1, s1r = ring_load(p1)
            s2, s2r = ring_load(p2)
            s3, s3r = ring_load(p3)
            v1 = s1.rearrange("p (h d) -> p h d", h=4)
            v2 = s2.rearrange("p (h d) -> p h d", h=4)
            v3 = s3.rearrange("p (h d) -> p h d", h=4)
            gi = l * 6 + 3
            for b in range(NBLK):
                tsl = slice(b * 512, (b + 1) * 512)
                for c in range(8):
                    dsl = slice(c * 128, (c + 1) * 128)
                    bk, bkr = next_bank()
                    for h in range(6):
                        wv, wr = (v1[:, h, dsl], s1r) if h < 4 else (v2[:, h - 4, dsl], s2r)
                        op("pe", "matmul", [wr, tokT_r[h]], [bkr], out=bk[:, :], lhsT=wv, rhs=tokT[:, h, tsl],
                           start=(h == 0), stop=False)
                    for a in range(4):
                        wv, wr = (v2[0:64, 2 + a, dsl], s2r) if a < 2 else (v3[0:64, a - 2, dsl], s3r)
                        op("pe", "matmul", [wr, memoT_r], [bkr], out=bk[:, :], lhsT=wv, rhs=memoT[0:64, a, tsl],
                           start=False, stop=(a == 3))
                    post.take(c, bk, bkr)
                post.finish(b, gi, False)

        def retention_layer(l):
            r = l // 2
            phase_barrier()
            AR.reset()
            hT = AR.alloc([128, 8, S], BF16); hT_r = Res("hT")
            tokT = AR.alloc([128, 6, S], BF16); tokT_r = [Res(f"tokT{h}") for h in range(6)]
            base_off = AR.off
            pre = PreNorm()
            for b in range(NBLK):
                pre.run(b, l * 6 + 2, hT[:, :, b * 512:(b + 1) * 512], hT_r)
            phase_barrier()
            AR.reset(base_off)
            if dbg.get("ret_stop", 99) == 1:
                return
            win = retwin_d[r].rearrange("(k p) f -> p k f", p=128)
            qT = AR.alloc([128, S], BF16); qT_r = Res("qT")
            kT = AR.alloc([128, S], BF16); kT_r = Res("kT")
            kwf = AR.alloc([128, 16, 128], BF16); kwf_r = Res("kwf")
            kwb = AR.alloc([128, 16, 128], BF16); kwb_r = Res("kwb")
            Sf = AR.alloc([128, 16, 128], BF16); Sf_r = Res("Sf")
            Sb = AR.alloc([128, 16, 128], BF16); Sb_r = Res("Sb")
            vh = AR.alloc([128, 16, 128], BF16); vh_r = Res("vh")
            tw = [AR.alloc([128, 2, 512], F32) for _ in range(2)]; tw_r = [Res(f"tw{i}") for i in range(2)]
            t1 = AR.alloc([128, 512], F32); t1_r = Res("t1")
            t2 = AR.alloc([128, 512], F32); t2_r = Res("t2")
            Mt = AR.alloc([128, 128], F32); Mt_r = Res("Mt")
            marg = AR.alloc([128, 128], F32); marg_r = Res("marg")
            wq = AR.alloc([128, 2, 128], F32); wq_r = Res("wq")
            hc = AR.alloc([128, 8], F32); hc_r = Res("hc")
            Sst = AR.alloc([128, 2, 128], F32); Sst_r = [Res("Sstf"), Res("Sstb")]
            pT = AR.alloc([128, 512], BF16); pT_r = Res("rpT")
            qw_f32 = AR.alloc([128, 512], F32); qw_r = Res("qw")
            qw = qw_f32.bitcast(BF16).rearrange("p (d q) -> p d q", d=2)
            A, A_r = t1, t1_r
            B, B_r = t2, t2_r
            yb2 = AR.alloc([128, 512], F32)
            yb2b = yb2.bitcast(BF16)
            ybf = yb2b[:, 0:512]; ybf_r = Res("ybf")
            ysq = yb2b[:, 512:1024]; ysq_r = Res("ysq")
            C = yb2
            G = AR.alloc([128, 512], BF16); G_r = Res("G")
            op("dve", "memset", [], [Sf_r], ap=Sf[:, 0, :], constant=0.0)
            op("dve", "memset", [], [Sb_r], ap=Sb[:, 15, :], constant=0.0)
            SC = 128.0 ** -0.5
            ntw = 0
            for h in range(dbg.get("ret_heads", 6)):
                lf = ldc[:, r * 12 + h:r * 12 + h + 1]
                lb = ldc[:, r * 12 + 6 + h:r * 12 + 6 + h + 1]
                op("act", "activation", [ccols_r, ldc_r], [hc_r], out=hc[:, 0:1], in_=ccols[:, 4:5], func=AF.Exp, scale=lf)
                op("act", "activation", [ccols_r, ldc_r], [hc_r], out=hc[:, 1:2], in_=ccols[:, 5:6], func=AF.Exp, scale=lb)
                op("act", "activation", [ldc_r], [hc_r], out=hc[:, 2:3], in_=lf, func=AF.Exp, scale=128.0)
                op("act", "activation", [ldc_r], [hc_r], out=hc[:, 3:4], in_=lb, func=AF.Exp, scale=128.0)
                op("dve", "tensor_scalar", [hc_r], [hc_r], out=hc[:, 0:2], in0=hc[:, 0:2], scalar1=SC, scalar2=None,
                   op0=ALU.mult)
                op("act", "activation", [ctabs_r, ldc_r], [wq_r], out=wq[:, 0, :], in_=ctabs[:, 0, :], func=AF.Exp, scale=lf)
                op("act", "activation", [ctabs_r, ldc_r], [wq_r], out=wq[:, 1, :], in_=ctabs[:, 1, :], func=AF.Exp, scale=lb)
                op("dve", "tensor_scalar", [ctabs_r, ldc_r], [marg_r], out=marg[:, :], in0=ctabs[:, 3, :], scalar1=lb,
                   scalar2=None, op0=ALU.mult)
                op("dve", "scalar_tensor_tensor", [ctabs_r, ldc_r, marg_r], [marg_r], out=marg[:, :], in0=ctabs[:, 2, :],
                   scalar=lf, in1=marg[:, :], op0=ALU.mult, op1=ALU.add)
                op("act", "activation", [marg_r], [Mt_r], out=Mt[:, :], in_=marg[:, :], func=AF.Exp)
                op("dve", "tensor_scalar", [Mt_r], [Mt_r], out=Mt[:, :], in0=Mt[:, :], scalar1=SC, scalar2=None,
                   op0=ALU.mult)
                if dbg.get("ret_stop", 99) == 2:
                    return
                def pA(slot_ap, h=h):
                    v = slot_ap.rearrange("p (k f) -> p k f", k=8)
                    qo = h * 128
                    ko = 768 + h * 128
                    return [(v[:, :, 0:128], win[:, :, qo:qo + 128]),
                            (v[:, :, 128:192], win[:, :, qo + 64:qo + 128]), (v[:, :, 192:256], win[:, :, qo:qo + 64]),
                            (v[:, :, 256:384], win[:, :, ko:ko + 128]),
                            (v[:, :, 384:448], win[:, :, ko + 64:ko + 128]), (v[:, :, 448:512], win[:, :, ko:ko + 64])]
                def pB(slot_ap, h=h):
                    v = slot_ap[:, 0:2048].rearrange("p (k f) -> p k f", k=8)
                    return [(v[:, :, 0:128], win[:, :, 2304 + h * 128:2304 + (h + 1) * 128]),
                            (v[:, :, 128:256], win[:, :, 1536 + h * 128:1536 + (h + 1) * 128])]
                wA, wA_r = ring_load(pA)
                wA = wA.rearrange("p (k f) -> p k f", k=8)
                wB, wB_r = ring_load(pB)
                wB = wB[:, 0:2048].rearrange("p (k f) -> p k f", k=8)
                for b in range(NBLK):
                    tsl = slice(b * 512, (b + 1) * 512)
                    k2 = ntw % 2
                    ntw += 1
                    load_tab_window(tw, tw_r, 0, b, k2)
                    for (dst, dst_r, off) in ((qT, qT_r, 0), (kT, kT_r, 256)):
                        b0, b0r = next_bank()
                        b1, b1r = next_bank()
                        for (bk, bkr, o2) in ((b0, b0r, off), (b1, b1r, off + 128)):
                            for k in range(8):
                                op("pe", "matmul", [wA_r, hT_r], [bkr], out=bk[:, :], lhsT=wA[:, k, o2:o2 + 128],
                                   rhs=hT[:, k, tsl], start=(k == 0), stop=(k == 7))
                        op("dve", "tensor_tensor", [b0r, tw_r[k2]], [t1_r], out=t1[:, :], in0=b0[:, :], in1=tw[k2][:, 0, :],
                           op=ALU.mult)
                        op("dve", "tensor_tensor", [b1r, tw_r[k2]], [t2_r], out=t2[:, :], in0=b1[:, :], in1=tw[k2][:, 1, :],
                           op=ALU.mult)
                        op("dve", "tensor_tensor", [t1_r, t2_r], [dst_r], out=dst[:, tsl], in0=t1[:, :], in1=t2[:, :],
                           op=ALU.add)
                if dbg.get("ret_stop", 99) == 3:
                    return
                for tq in range(4):
                    bk, bkr = next_bank()
                    for j in range(4):
                        t = tq * 4 + j
                        for k in range(8):
                            op("pe", "matmul", [wB_r, hT_r], [bkr], out=bk[:, j * 128:(j + 1) * 128],
                               lhsT=hT[:, k, t * 128:(t + 1) * 128], rhs=wB[:, k, 128:256], start=(k == 0), stop=(k == 7))
                    op("act", "copy", [bkr], [vh_r], out=vh[:, tq * 4:tq * 4 + 4, :],
                       in_=bk[:, :].rearrange("p (j q) -> p j q", j=4))
                if dbg.get("ret_stop", 99) == 4:
                    return
                for tq in range(4):
                    bk, bkr = next_bank()
                    for j in range(4):
                        n = tq * 4 + j
                        op("pe", "matmul", [kT_r, ones_r], [bkr], out=bk[:, j * 128:(j + 1) * 128],
                           lhsT=kT[:, n * 128:(n + 1) * 128], rhs=ident_bf[:, :], start=True, stop=True)
                    bv = bk[:, :].rearrange("p (j q) -> p j q", j=4)
                    op("dve", "tensor_scalar", [bkr, hc_r], [kwf_r], out=kwf[:, tq * 4:tq * 4 + 4, :], in0=bv,
                       scalar1=hc[:, 0:1], scalar2=None, op0=ALU.mult)
                    op("dve", "tensor_scalar", [bkr, hc_r], [kwb_r], out=kwb[:, tq * 4:tq * 4 + 4, :], in0=bv,
                       scalar1=hc[:, 1:2], scalar2=None, op0=ALU.mult)
                if dbg.get("ret_stop", 99) == 5:
                    return
                fb = [next_bank() for _ in range(4)]
                for n in range(15):
                    bk, bkr = fb[n // 4]
                    op("pe", "matmul", [kwf_r, vh_r], [bkr], out=bk[:, (n % 4) * 128:(n % 4 + 1) * 128],
                       lhsT=kwf[:, n, :], rhs=vh[:, n, :], start=True, stop=True)
                bb = [next_bank() for _ in range(4)]
                for n in range(15, 0, -1):
                    bk, bkr = bb[n // 4]
                    op("pe", "matmul", [kwb_r, vh_r], [bkr], out=bk[:, (n % 4) * 128:(n % 4 + 1) * 128],
                       lhsT=kwb[:, n, :], rhs=vh[:, n, :], start=True, stop=True)
                for n in range(15):
                    bk, bkr = fb[n // 4]
                    kv = bk[:, (n % 4) * 128:(n % 4 + 1) * 128]
                    if n == 0:
                        op("dve", "tensor_copy", [bkr], [Sst_r[0]], out=Sst[:, 0, :], in_=kv)
                    else:
                        op("dve", "scalar_tensor_tensor", [bkr, hc_r, Sst_r[0]], [Sst_r[0]], out=Sst[:, 0, :],
                           in0=Sst[:, 0, :], scalar=hc[:, 2:3], in1=kv, op0=ALU.mult, op1=ALU.add)
                    op("act", "copy", [Sst_r[0]], [Sf_r], out=Sf[:, n + 1, :], in_=Sst[:, 0, :])
                for n in range(15, 0, -1):
                    bk, bkr = bb[n // 4]
                    kv = bk[:, (n % 4) * 128:(n % 4 + 1) * 128]
                    if n == 15:
                        op("dve", "tensor_copy", [bkr], [Sst_r[1]], out=Sst[:, 1, :], in_=kv)
                    else:
                        op("dve", "scalar_tensor_tensor", [bkr, hc_r, Sst_r[1]], [Sst_r[1]], out=Sst[:, 1, :],
                           in0=Sst[:, 1, :], scalar=hc[:, 3:4], in1=kv, op0=ALU.mult, op1=ALU.add)
                    op("act", "copy", [Sst_r[1]], [Sb_r], out=Sb[:, n - 1, :], in_=Sst[:, 1, :])
                if dbg.get("ret_stop", 99) == 6:
                    return
                fr = {}

                def front(b, h=h):
                    tsl = slice(b * 512, (b + 1) * 512)
                    bs, bsr = next_bank()
                    for j in range(4):
                        n = b * 4 + j
                        csl = slice(n * 128, (n + 1) * 128)
                        op("pe", "matmul", [kT_r, qT_r], [bsr], out=bs[:, j * 128:(j + 1) * 128], lhsT=kT[:, csl],
                           rhs=qT[:, csl], start=True, stop=True)
                    op("dve", "tensor_tensor", [bsr, Mt_r], [pT_r], out=pT[:, :].rearrange("p (j q) -> p j q", j=4),
                       in0=bs[:, :].rearrange("p (j q) -> p j q", j=4),
                       in1=Mt[:, :].unsqueeze(1).to_broadcast([128, 4, 128]), op=ALU.mult)
                    for d in range(2):
                        op("dve", "tensor_tensor", [qT_r, wq_r], [qw_r],
                           out=qw[:, d, :].rearrange("p (j q) -> p j q", j=4),
                           in0=qT[:, tsl].rearrange("p (j q) -> p j q", j=4),
                           in1=wq[:, d, :].unsqueeze(1).to_broadcast([128, 4, 128]), op=ALU.mult)
                    bg, bgr = next_bank()
                    for k in range(8):
                        op("pe", "matmul", [wB_r, hT_r], [bgr], out=bg[:, :], lhsT=wB[:, k, 0:128], rhs=hT[:, k, tsl],
                           start=(k == 0), stop=(k == 7))
                    by, byr = next_bank()
                    for j in range(4):
                        n = b * 4 + j
                        jsl = slice(j * 128, (j + 1) * 128)
                        op("pe", "matmul", [vh_r, pT_r], [byr], out=by[:, jsl], lhsT=vh[:, n, :], rhs=pT[:, jsl],
                           start=True, stop=False)
                        op("pe", "matmul", [Sf_r, qw_r], [byr], out=by[:, jsl], lhsT=Sf[:, n, :], rhs=qw[:, 0, jsl],
                           start=False, stop=False)
                        op("pe", "matmul", [Sb_r, qw_r], [byr], out=by[:, jsl], lhsT=Sb[:, n, :], rhs=qw[:, 1, jsl],
                           start=False, stop=True)
                    fr[b] = (bg, bgr, by, byr)

                def back(b, h=h):
                    tsl = slice(b * 512, (b + 1) * 512)
                    bg, bgr, by, byr = fr.pop(b)
                    op("act", "activation", [bgr], [G_r], out=G[:, :], in_=bg[:, :], func=AF.Silu)
                    op("act", "copy", [byr], [A_r], out=A[:, :], in_=by[:, :])
                    op("act", "copy", [byr], [ybf_r], out=ybf[:, :], in_=by[:, :])
                    op("act", "activation", [byr], [ysq_r], out=ysq[:, :], in_=by[:, :], func=AF.Square)
                    bm, bmr = next_bank()
                    bq, bqr = next_bank()
                    op("pe", "matmul", [ones_r, ybf_r], [bmr], out=bm[:, :], lhsT=ones_bf[:, :], rhs=ybf[:, :],
                       start=True, stop=True)
                    op("pe", "matmul", [ones_r, ysq_r], [bqr], out=bq[:, :], lhsT=ones_bf[:, :], rhs=ysq[:, :],
                       start=True, stop=True)
                    op("act", "mul", [bmr], [B_r], out=B[:, :], in_=bm[:, :], mul=1.0 / 128)
                    op("dve", "tensor_tensor", [A_r, B_r], [ybf_r, ysq_r], out=C[:, :], in0=A[:, :], in1=B[:, :],
                       op=ALU.subtract)
                    op("act", "activation", [B_r], [A_r], out=A[:, :], in_=B[:, :], func=AF.Square)
                    op("dve", "scalar_tensor_tensor", [bqr, A_r], [B_r], out=B[:, :], in0=bq[:, :], scalar=1.0 / 128,
                       in1=A[:, :], op0=ALU.mult, op1=ALU.subtract)
                    op("act", "activation", [B_r, eps_r], [A_r], out=A[:, :], in_=B[:, :], func=AF.Ln,
                       bias=eps_col[:, 0:1], scale=1.0)
                    op("act", "activation", [A_r], [B_r], out=B[:, :], in_=A[:, :], func=AF.Exp, scale=-0.5)
                    op("dve", "scalar_tensor_tensor", [ybf_r, ysq_r, vcol_r, B_r], [A_r], out=A[:, :], in0=C[:, :],
                       scalar=vcol[:, r * 6 + h:r * 6 + h + 1], in1=B[:, :], op0=ALU.mult, op1=ALU.mult)
                    op("dve", "tensor_tensor", [A_r, G_r], [tokT_r[h]], out=tokT[:, h, tsl], in0=A[:, :], in1=G[:, :],
                       op=ALU.mult)
                front(0)
                for b in range(NBLK):
                    if b + 1 < NBLK:
                        front(b + 1)
                    back(b)
            if dbg.get("ret_stop", 99) == 8:
                return
            phase_barrier()
            AR.reset(base_off)
            memoT = AR.alloc([128, 4, S], BF16); memoT_r = Res("memoT")
            off2 = AR.off

            def wqm_fn(slot_ap):
                v = slot_ap[:, 0:2048].rearrange("p (k f) -> p k f", k=8)
                return [(v[:, :, :], win[:, :, 3072:3328])]
            mem_attention(l, hT, hT_r, wqm_fn, memoT, memoT_r)
            phase_barrier()
            AR.reset(off2)
            post = PostNorm()
            out_proj(l, tokT, tokT_r, memoT, memoT_r, post)

        def mla_layer(l):
            r = l // 2
            phase_barrier()
            AR.reset()
            hT = AR.alloc([128, 8, S], BF16); hT_r = Res("hT")
            tokT = AR.alloc([128, 6, S], BF16); tokT_r = [Res(f"tokT{h}") for h in range(6)]
            base_off = AR.off
            pre = PreNorm()
            for b in range(NBLK):
                pre.run(b, l * 6 + 2, hT[:, :, b * 512:(b + 1) * 512], hT_r)
            phase_barrier()
            AR.reset(base_off)
            win = mlawin_d[r].rearrange("(k p) f -> p k f", p=128)
            cqn = AR.alloc([128, 2, S], BF16); cqn_r = Res("cqn")
            ckvn = AR.alloc([128, S], BF16); ckvn_r = Res("ckvn")
            krT = AR.alloc([128, S], BF16); krT_r = Res("krT")
            qT = AR.alloc([128, S], BF16); qT_r = Res("mqT")
            kT = AR.alloc([128, S], BF16); kT_r = Res("mkT_h")
            vh = AR.alloc([128, 16, 128], BF16); vh_r = Res("mvh")
            tw = [AR.alloc([128, 2, 512], F32) for _ in range(2)]; tw_r = [Res(f"mtw{i}") for i in range(2)]
            t1 = AR.alloc([128, 512], F32); t1_r = Res("mt1")
            t2 = AR.alloc([128, 512], F32); t2_r = Res("mt2")
            sq3 = AR.alloc([128, 3, 512], BF16); sq3_r = Res("sq3")
            rtmp = AR.alloc([128, 512], F32); rtmp_r = Res("mrtmp")
            rs = AR.alloc([128, 512], F32); rs_r = Res("mrs")
            NPT = 3
            pT = [AR.alloc([128, 512], BF16) for _ in range(NPT)]; pT_r = [Res(f"apT{i}") for i in range(NPT)]
            rden, rden_r = rtmp, rtmp_r

            def pA(slot_ap):
                v = slot_ap[:, 0:3584].rearrange("p (k f) -> p k f", k=8)
                return [(v[:, :, 0:416], win[:, :, 0:416]),
                        (v[:, :, 416:432], win[:, :, 400:416]), (v[:, :, 432:448], win[:, :, 384:400])]
            wA, wA_r = ring_load(pA)
            wA = wA[:, 0:3584].rearrange("p (k f) -> p k f", k=8)
            ntw = 0
            for b in range(NBLK):
                tsl = slice(b * 512, (b + 1) * 512)
                k2 = ntw % 2
                ntw += 1
                load_tab_window(tw, tw_r, 1, b, k2)
                bq = [next_bank() for _ in range(3)]
                for ci in range(3):
                    bk, bkr = bq[ci]
                    for k in range(8):
                        op("pe", "matmul", [wA_r, hT_r], [bkr], out=bk[:, :], lhsT=wA[:, k, ci * 128:(ci + 1) * 128],
                           rhs=hT[:, k, tsl], start=(k == 0), stop=(k == 7))
                    op("act", "activation", [bkr], [sq3_r], out=sq3[:, ci, :], in_=bk[:, :], func=AF.Square)
                rstd_from_sq(sq3[:, 0:2, :], [sq3_r], 2, 1.0 / 256, rtmp, rtmp_r, rs, rs_r)
                for ci in range(2):
                    bk, bkr = bq[ci]
                    op("dve", "scalar_tensor_tensor", [bkr, vcol_r, rs_r], [cqn_r], out=cqn[:, ci, tsl], in0=bk[:, :],
                       scalar=vcol[:, 12 + r * 2 + ci:13 + r * 2 + ci], in1=rs[:, :], op0=ALU.mult, op1=ALU.mult)
                rstd_from_sq(sq3[:, 2:3, :], [sq3_r], 1, 1.0 / 128, rtmp, rtmp_r, rs, rs_r)
                bk, bkr = bq[2]
                op("dve", "scalar_tensor_tensor", [bkr, vcol_r, rs_r], [ckvn_r], out=ckvn[:, tsl], in0=bk[:, :],
                   scalar=vcol[:, 16 + r:17 + r], in1=rs[:, :], op0=ALU.mult, op1=ALU.mult)
                b0, b0r = next_bank()
                b1, b1r = next_bank()
                for (bk, bkr, o2) in ((b0, b0r, 384), (b1, b1r, 416)):
                    for k in range(8):
                        op("pe", "matmul", [wA_r, hT_r], [bkr], out=bk[0:32, :], lhsT=wA[:, k, o2:o2 + 32],
                           rhs=hT[:, k, tsl], start=(k == 0), stop=(k == 7))
                op("dve", "tensor_tensor", [b0r, tw_r[k2]], [t1_r], out=t1[0:32, :], in0=b0[0:32, :],
                   in1=tw[k2][0:32, 0, :], op=ALU.mult)
                op("dve", "tensor_tensor", [b1r, tw_r[k2]], [t2_r], out=t2[0:32, :], in0=b1[0:32, :],
                   in1=tw[k2][0:32, 1, :], op=ALU.mult)
                op("dve", "tensor_tensor", [t1_r, t2_r], [krT_r], out=krT[0:32, tsl], in0=t1[0:32, :], in1=t2[0:32, :],
                   op=ALU.add)
            wuq = mlawuq_d[r].rearrange("(k p) (h f) -> p k h f", p=128, f=96)
            def pQ(slot_ap):
                v = slot_ap[:, 0:2304].rearrange("p (k two h f) -> p k two h f", k=2, two=2, h=6)
                prs = []
                for k in range(2):
                    prs += [(v[:, k, 0, :, :], wuq[:, k, :, :]),
                            (v[:, k, 1, :, 0:64], wuq[:, k, :, 0:64]),
                            (v[:, k, 1, :, 64:80], wuq[:, k, :, 80:96]), (v[:, k, 1, :, 80:96], wuq[:, k, :, 64:80])]
                return prs
            wQ, wQ_r = ring_load(pQ)
            wQ = wQ[:, 0:2304].rearrange("p (k two h f) -> p k two h f", k=2, two=2, h=6)
            def pKV(slot_ap):
                return [(slot_ap[:, 0:1152], mlawukv_d[r])]
            wKV, wKV_r = ring_load(pKV)
            SCALE = 96.0 ** -0.5
            npt = 0
            nit = 0
            for h in range(6):
                for b in range(NBLK):
                    tsl = slice(b * 512, (b + 1) * 512)
                    k2 = ntw % 2
                    ntw += 1
                    load_tab_window(tw, tw_r, 1, b, k2)
                    b0, b0r = next_bank()
                    b1, b1r = next_bank()
                    for (bk, bkr, two) in ((b0, b0r, 0), (b1, b1r, 1)):
                        for k in range(2):
                            op("pe", "matmul", [wQ_r, cqn_r], [bkr], out=bk[0:96, :], lhsT=wQ[:, k, two, h, :],
                               rhs=cqn[:, k, tsl], start=(k == 0), stop=(k == 1))
                    op("act", "copy", [b0r], [qT_r], out=qT[0:64, tsl], in_=b0[0:64, :])
                    op("dve", "tensor_tensor", [b0r, tw_r[k2]], [t1_r], out=t1[64:96, :], in0=b0[64:96, :],
                       in1=tw[k2][64:96, 0, :], op=ALU.mult)
                    op("dve", "tensor_tensor", [b1r, tw_r[k2]], [t2_r], out=t2[64:96, :], in0=b1[64:96, :],
                       in1=tw[k2][64:96, 1, :], op=ALU.mult)
                    op("dve", "tensor_tensor", [t1_r, t2_r], [qT_r], out=qT[64:96, tsl], in0=t1[64:96, :],
                       in1=t2[64:96, :], op=ALU.add)
                    bk, bkr = next_bank()
                    op("pe", "matmul", [wKV_r, ckvn_r], [bkr], out=bk[0:64, :], lhsT=wKV[:, h * 192:h * 192 + 64],
                       rhs=ckvn[:, tsl], start=True, stop=True)
                    op("act", "copy", [bkr], [kT_r], out=kT[0:64, tsl], in_=bk[0:64, :])
                sc.dma("sp", kT_s, [(kT[64:96, :], krT[0:32, :])], reads=[krT_r], writes=[kT_r])
                for tq in range(4):
                    bk, bkr = next_bank()
                    for j in range(4):
                        t = tq * 4 + j
                        op("pe", "matmul", [wKV_r, ckvn_r], [bkr], out=bk[:, j * 128:(j + 1) * 128],
                           lhsT=ckvn[:, t * 128:(t + 1) * 128], rhs=wKV[:, h * 192 + 64:h * 192 + 192],
                           start=True, stop=True)
                    op("act", "copy", [bkr], [vh_r], out=vh[:, tq * 4:tq * 4 + 4, :],
                       in_=bk[:, :].rearrange("p (j q) -> p j q", j=4))
                for b in range(NBLK):
                    tsl = slice(b * 512, (b + 1) * 512)
                    ni = 4 + 2 * (nit % 2)
                    nit += 1
                    bn, bnr = banks[ni], bank_r[ni]
                    bd, bdr = banks[ni + 1], bank_r[ni + 1]
                    def s_mm(kc, npt0=npt):
                        si = (npt0 + kc) % 4
                        bs, bsr = banks[si], bank_r[si]
                        op("pe", "matmul", [kT_r, qT_r], [bsr], out=bs[:, :], lhsT=kT[0:96, kc * 128:(kc + 1) * 128],
                           rhs=qT[0:96, tsl], start=True, stop=True)
                    s_mm(0)
                    for kc in range(16):
                        si = (npt + kc) % 4
                        pi = (npt + kc) % NPT
                        bs, bsr = banks[si], bank_r[si]
                        if kc + 1 < 16:
                            s_mm(kc + 1)
                        op("act", "activation", [bsr], [pT_r[pi]], out=pT[pi][:, :], in_=bs[:, :], func=AF.Exp,
                           scale=SCALE)
                        op("pe", "matmul", [vh_r, pT_r[pi]], [bnr], out=bn[:, :], lhsT=vh[:, kc, :], rhs=pT[pi][:, :],
                           start=(kc == 0), stop=(kc == 15))
                        op("pe", "matmul", [ones_r, pT_r[pi]], [bdr], out=bd[:, :], lhsT=ones_bf[:, :], rhs=pT[pi][:, :],
                           start=(kc == 0), stop=(kc == 15))
                    npt += 16
                    op("dve", "reciprocal", [bdr], [rden_r], out=rden[:, :], in_=bd[:, :])
                    op("dve", "tensor_tensor", [bnr, rden_r], [tokT_r[h]], out=tokT[:, h, tsl], in0=bn[:, :],
                       in1=rden[:, :], op=ALU.mult)
            bank_n[0] = 0
            phase_barrier()
            AR.reset(base_off)
            memoT = AR.alloc([128, 4, S], BF16); memoT_r = Res("memoT")
            off2 = AR.off

            def wqm_fn(slot_ap):
                v = slot_ap[:, 0:2048].rearrange("p (k f) -> p k f", k=8)
                return [(v[:, :, :], win[:, :, 416:672])]
            mem_attention(l, hT, hT_r, wqm_fn, memoT, memoT_r)
            phase_barrier()
            AR.reset(off2)
            post = PostNorm()
            out_proj(l, tokT, tokT_r, memoT, memoT_r, post)

        for l in range(n_layers):
            if dbg.get("ffn1", True):
                ffn(l, 0)
            if dbg.get("mixer", True):
                if l % 2 == 0:
                    retention_layer(l)
                else:
                    mla_layer(l)
            if dbg.get("ffn2", True):
                ffn(l, 1)

        phase_barrier()
        AR.reset()
        xstage = [AR.alloc([128, D], F32) for _ in range(2)]
        xstage_r = [Res(f"xsto{i}") for i in range(2)]
        for t in range(16):
            si = t % 2
            for half in range(2):
                bk, bkr = next_bank()
                for j in range(4):
                    c = half * 4 + j
                    op("pe", "transpose", [xT_r[(c, t // 4)], ident_r], [bkr], out=bk[:, j * 128:(j + 1) * 128],
                       in_=xT[:, c, t * 128:(t + 1) * 128], identity=ident[:, :])
                if half == 0:
                    op("act", "copy", [bkr], [xstage_r[si]], out=xstage[si][:, 0:512], in_=bk[:, :])
                else:
                    op("dve", "tensor_copy", [bkr], [xstage_r[si]], out=xstage[si][:, 512:1024], in_=bk[:, :])
            sc.dma("sp", xsto_s[si], [(out_d[t * 128:(t + 1) * 128, :], xstage[si][:, :])], reads=[xstage_r[si]])
        for ds in xsto_s:
            sc._wait("sp", ds, ds.total)

        for e in Sched.ENG:
            sc.replay(e, None)
        with nc.Block() as block:
            @block.tensor
            def _(eng):
                sc.emit("pe", eng)

            @block.scalar
            def _(eng):
                sc.emit("act", eng)

            @block.vector
            def _(eng):
                sc.emit("dve", eng)

            @block.gpsimd
            def _(eng):
                sc.emit("pool", eng)

            @block.sync
            def _(eng):
                sc.emit("sp", eng)
    nc._used_inputs = list(used.keys())
    nc._arena_peak = AR.peak
    return nc


def _consts():
    p = np.arange(128)
    cols = np.zeros((128, 8), np.float32)
    cols[:, 0] = (1.0 / (10000.0 ** (np.arange(0, 128, 2, dtype=np.float32) / 128.0)))[p % 64]
    cols[:, 1] = (1.0 / (10000.0 ** (np.arange(0, 32, 2, dtype=np.float32) / 32.0)))[p % 16]
    cols[:, 2] = np.where(p < 64, -1.0, 1.0)
    cols[:, 3] = np.where((p % 32) < 16, -1.0, 1.0)
    cols[:, 4] = 127.0 - p
    cols[:, 5] = p
    tabs = np.zeros((128, 4, 128), np.float32)
    tl = np.arange(128, dtype=np.float32)
    tabs[:, 0, :] = tl[None, :] + 1.0
    tabs[:, 1, :] = 128.0 - tl[None, :]
    d = tl[None, :] - p[:, None].astype(np.float32)
    tabs[:, 2, :] = np.maximum(d, 0.0)
    tabs[:, 3, :] = np.maximum(-d, 0.0)
    return {"c_ident": np.eye(128, dtype=np.float32), "c_cols": cols, "c_tabs": tabs}


def kernel(x, mem, positions, norm_gains, ffn_w_gu, ffn_w_down, w_o, mem_norm, mem_w_kv,
           ret_w_in, ret_log_decay, ret_head_norm, mla_w_in, mla_q_norm, mla_kv_norm,
           mla_w_uq, mla_w_ukv, _dbg=None):
    f = lambda a: np.ascontiguousarray(np.asarray(a, dtype=np.float32))
    shared = {
        "norm_gains": f(norm_gains).reshape(DEPTH * 6, D),
        "ffn_w_gu": f(ffn_w_gu), "ffn_w_down": f(ffn_w_down), "w_o": f(w_o),
        "mem_norm": f(mem_norm).reshape(8, 128), "mem_w_kv": f(mem_w_kv),
        "ret_w_in": f(ret_w_in), "ret_log_decay": f(ret_log_decay).reshape(1, 24),
        "ret_head_norm": f(ret_head_norm).reshape(12, 128),
        "mla_w_in": f(mla_w_in), "mla_q_norm": f(mla_q_norm).reshape(4, 128),
        "mla_kv_norm": f(mla_kv_norm).reshape(2, 128),
        "mla_w_uq": f(mla_w_uq), "mla_w_ukv": f(mla_w_ukv),
    }
    shared.update(_consts())
    x = f(x)
    mem = f(mem)
    pos = np.ascontiguousarray(np.asarray(positions, dtype=np.int32))
    nc = build_program(_dbg)
    if _dbg and "nl_w" in _dbg:
        for k in ("ffn_w_gu", "ffn_w_down", "w_o", "mem_w_kv"):
            shared[k] = np.ascontiguousarray(shared[k][:_dbg["nl_w"]])
    in_maps = []
    for b in range(8):
        m = {k: v for k, v in shared.items() if k in nc._used_inputs}
        m["x"] = x[b]
        if "mem" in nc._used_inputs:
            m["mem"] = mem[b]
        if "positions" in nc._used_inputs:
            m["positions"] = pos[b].reshape(1, S)
        in_maps.append(m)
    res = run_bass_kernel_spmd(nc, in_maps, core_ids=list(range(8)))
    return np.stack([np.asarray(r["out"], dtype=np.float32) for r in res.results], axis=0)
```
